# Optimizing a Trainium2 kernel written in Bass

```python
import math
import jax, jax.numpy as jnp
from jax import lax
import numpy as np

D_MODEL = 2048
BATCH = 4
SEQ = 4096
DEPTH = 4

HEAD_DIM = 128
H_FOX = 6
H_DSA = 6
H_MEM = 4
D_FOX = H_FOX * HEAD_DIM
D_DSA = H_DSA * HEAD_DIM
D_MEMG = H_MEM * HEAD_DIM
D_MIX = D_FOX + D_DSA + D_MEMG
IDX_HEADS = 16
IDX_DIM = 64
INDEX_TOPK = 256
N_MEM = 256
Q_BLOCK = 128
ROPE_THETA = 10000.0
LN_EPS = 1e-5
ALPHA = (2 * DEPTH) ** 0.25
BETA = (8 * DEPTH) ** -0.25
NEG = -1e30

SPLITS = (
    D_FOX, D_FOX, D_FOX, D_FOX, H_FOX,
    D_DSA, D_DSA, D_DSA, D_DSA,
    IDX_HEADS * IDX_DIM, IDX_DIM, IDX_HEADS,
    D_MEMG, D_MEMG,
)
N_IN = sum(SPLITS)

kernel_name = "hybrid_fox_dsa_memory_deepnorm"


def rope(x, pos):
    d = x.shape[-1]
    half = d // 2
    inv_freq = ROPE_THETA ** (-jnp.arange(half, dtype=jnp.float32) / half)
    ang = pos.astype(jnp.float32)[:, None] * inv_freq[None, :]
    cos = jnp.cos(ang)[None, :, None, :].astype(x.dtype)
    sin = jnp.sin(ang)[None, :, None, :].astype(x.dtype)
    x1, x2 = x[..., :half], x[..., half:]
    return jnp.concatenate([x1 * cos - x2 * sin, x2 * cos + x1 * sin], axis=-1)


def to_blocks(a):
    b, s = a.shape[0], a.shape[1]
    return jnp.moveaxis(a.reshape(b, s // Q_BLOCK, Q_BLOCK, *a.shape[2:]), 1, 0)


def from_blocks(a):
    a = jnp.moveaxis(a, 0, 1)
    return a.reshape(a.shape[0], a.shape[1] * a.shape[2], *a.shape[3:])


def fox_attention(q, k, v, cum_logf):
    b, s, h, d = q.shape
    scale = d ** -0.5
    key_pos = jnp.arange(s)
    c_keys = jnp.transpose(cum_logf, (0, 2, 1))
    q_pos = key_pos.reshape(s // Q_BLOCK, Q_BLOCK)

    def one_block(args):
        qb, cb, tb = args
        sc = jnp.einsum('bqhd,bkhd->bhqk', qb, k).astype(jnp.float32) * scale
        sc = sc + (jnp.transpose(cb, (0, 2, 1))[:, :, :, None] - c_keys[:, :, None, :])
        causal = tb[:, None] >= key_pos[None, :]
        sc = jnp.where(causal[None, None], sc, NEG)
        p = jax.nn.softmax(sc, axis=-1).astype(v.dtype)
        return jnp.einsum('bhqk,bkhd->bqhd', p, v)

    out = lax.map(one_block, (to_blocks(q), to_blocks(cum_logf), q_pos))
    return from_blocks(out)


def dsa_attention(q, k, v, q_idx, k_idx, w_idx):
    b, s, h, d = q.shape
    topk = min(INDEX_TOPK, s // 4)
    scale = d ** -0.5
    idx_scale = IDX_DIM ** -0.5
    key_pos = jnp.arange(s)
    q_pos = key_pos.reshape(s // Q_BLOCK, Q_BLOCK)

    def one_block(args):
        qb, qib, wb, tb = args
        rel = jax.nn.relu(jnp.einsum('bqhi,bki->bqhk', qib, k_idx).astype(jnp.float32) * idx_scale)
        score = jnp.einsum('bqhk,bqh->bqk', rel, wb.astype(jnp.float32))
        causal = tb[:, None] >= key_pos[None, :]
        score = jnp.where(causal[None], score, NEG)
        _, sel = lax.top_k(score, topk)
        valid = sel <= tb[None, :, None]
        kg = jax.vmap(lambda kk, ii: kk[ii])(k, sel)
        vg = jax.vmap(lambda vv, ii: vv[ii])(v, sel)
        sc = jnp.einsum('bqhd,bqkhd->bhqk', qb, kg).astype(jnp.float32) * scale
        sc = jnp.where(jnp.transpose(valid, (0, 1, 2))[:, None], sc, NEG)
        p = jax.nn.softmax(sc, axis=-1).astype(vg.dtype)
        return jnp.einsum('bhqk,bqkhd->bqhd', p, vg)

    out = lax.map(one_block, (to_blocks(q), to_blocks(q_idx), to_blocks(w_idx), q_pos))
    return from_blocks(out)


def memory_attention(q, k_mem, v_mem):
    scale = q.shape[-1] ** -0.5
    sc = jnp.einsum('bshd,bmhd->bhsm', q, k_mem).astype(jnp.float32) * scale
    p = jax.nn.softmax(sc, axis=-1).astype(v_mem.dtype)
    return jnp.einsum('bhsm,bmhd->bshd', p, v_mem)


def layer_norm(z, g, b):
    zf = z.astype(jnp.float32)
    mu = jnp.mean(zf, axis=-1, keepdims=True)
    var = jnp.mean(jnp.square(zf - mu), axis=-1, keepdims=True)
    out = (zf - mu) * lax.rsqrt(var + LN_EPS) * g.astype(jnp.float32) + b.astype(jnp.float32)
    return out.astype(z.dtype)


def hybrid_layer(x, mem, w_in, b_forget, w_mem_kv, w_out, ln_gain, ln_bias):
    b, s, _ = x.shape
    pos = jnp.arange(s)
    h = jnp.einsum('bsd,dn->bsn', x, w_in)
    offsets = np.cumsum(SPLITS)[:-1].tolist()
    (fq, fk, fv, fg, f_logit,
     dq, dk, dv, dg,
     iq, ik, iw,
     mq, mg) = jnp.split(h, offsets, axis=-1)

    heads = lambda t, nh: t.reshape(b, s, nh, HEAD_DIM)

    log_f = jax.nn.log_sigmoid(f_logit.astype(jnp.float32) + b_forget.astype(jnp.float32))
    cum_logf = jnp.cumsum(log_f, axis=1)
    y_fox = fox_attention(heads(fq, H_FOX), heads(fk, H_FOX), heads(fv, H_FOX), cum_logf)
    y_fox = y_fox.reshape(b, s, D_FOX) * jax.nn.silu(fg)

    q_idx = rope(iq.reshape(b, s, IDX_HEADS, IDX_DIM), pos)
    k_idx = rope(ik.reshape(b, s, 1, IDX_DIM), pos)[:, :, 0, :]
    w_idx = iw * (IDX_HEADS ** -0.5)
    y_dsa = dsa_attention(rope(heads(dq, H_DSA), pos), rope(heads(dk, H_DSA), pos), heads(dv, H_DSA),
                          q_idx, k_idx, w_idx)
    y_dsa = y_dsa.reshape(b, s, D_DSA) * jax.nn.silu(dg)

    kv = jnp.einsum('bmd,dn->bmn', mem, w_mem_kv)
    k_mem, v_mem = jnp.split(kv, 2, axis=-1)
    m = mem.shape[1]
    y_mem = memory_attention(heads(mq, H_MEM), k_mem.reshape(b, m, H_MEM, HEAD_DIM),
                             v_mem.reshape(b, m, H_MEM, HEAD_DIM))
    y_mem = y_mem.reshape(b, s, D_MEMG) * jax.nn.silu(mg)

    y = jnp.concatenate([y_fox, y_dsa, y_mem], axis=-1)
    y = jnp.einsum('bsn,nd->bsd', y, w_out)
    return layer_norm(ALPHA * x + y, ln_gain, ln_bias)


def setup_inputs(seed: int = 0) -> dict:
    key = jax.random.key(seed)
    ks = jax.random.split(key, 8)
    x = jax.random.normal(ks[0], (BATCH, SEQ, D_MODEL), jnp.float32)
    mem = jax.random.normal(ks[1], (BATCH, N_MEM, D_MODEL), jnp.float32)
    w_in = jax.random.normal(ks[2], (DEPTH, D_MODEL, N_IN), jnp.float32) * D_MODEL ** -0.5
    b_forget = 3.0 + 0.5 * jax.random.normal(ks[3], (DEPTH, H_FOX), jnp.float32)
    w_mem_kv = jax.random.normal(ks[4], (DEPTH, D_MODEL, 2 * D_MEMG), jnp.float32) * D_MODEL ** -0.5
    w_out = jax.random.normal(ks[5], (DEPTH, D_MIX, D_MODEL), jnp.float32) * (D_MIX ** -0.5) * BETA
    ln_gain = 1.0 + 0.05 * jax.random.normal(ks[6], (DEPTH, D_MODEL), jnp.float32)
    ln_bias = 0.02 * jax.random.normal(ks[7], (DEPTH, D_MODEL), jnp.float32)
    return {"x": x, "mem": mem, "w_in": w_in, "b_forget": b_forget, "w_mem_kv": w_mem_kv,
            "w_out": w_out, "ln_gain": ln_gain, "ln_bias": ln_bias}


def reference(x, mem, w_in, b_forget, w_mem_kv, w_out, ln_gain, ln_bias):
    h = x
    for l in range(DEPTH):
        h = hybrid_layer(h, mem, w_in[l], b_forget[l], w_mem_kv[l], w_out[l], ln_gain[l], ln_bias[l])
    return h
```

```python
from contextlib import ExitStack
from concourse.bass_utils import run_bass_kernel_spmd
import numpy as np
import concourse.bass as bass
import concourse.mybir as mybir

F32 = mybir.dt.float32
BF16 = mybir.dt.bfloat16
ALU = mybir.AluOpType
AF = mybir.ActivationFunctionType

COMPUTE = ("tensor", "vector", "scalar", "gpsimd")


class Buf:
    __slots__ = ("ap", "w", "r", "name")

    def __init__(self, ap, name=""):
        self.ap = ap
        self.w = None
        self.r = []
        self.name = name

    def __getitem__(self, idx):
        return V(self.ap[idx], self)

    @property
    def v(self):
        return V(self.ap, self)


class V:
    __slots__ = ("ap", "buf")

    def __init__(self, ap, buf):
        self.ap = ap
        self.buf = buf

    def __getitem__(self, idx):
        return V(self.ap[idx], self.buf)

    def re(self, pattern, **kw):
        return V(self.ap.rearrange(pattern, **kw), self.buf)

    def bc(self, shape):
        return V(self.ap.to_broadcast(shape), self.buf)


def _ap(x):
    return x.ap if isinstance(x, V) else x


class Op:
    __slots__ = ("eng", "fn", "deps", "idx", "needed", "is_dma", "semval", "dsem", "is_cc")

    def __init__(self, eng, fn, is_dma):
        self.eng = eng
        self.fn = fn
        self.deps = set()
        self.needed = False
        self.is_dma = is_dma
        self.semval = None
        self.dsem = None
        self.is_cc = False


class Rec:
    def __init__(self, nc, n_dma_sems=12):
        self.nc = nc
        self.ops = []
        self.eng_ops = {e: [] for e in ("tensor", "vector", "scalar", "gpsimd", "sync")}
        self.n_dma_sems = n_dma_sems
        self.last_real = {}
        self.dma_since = []

    def emit(self, eng, fn, reads=(), writes=(), is_dma=False):
        op = Op(eng, fn, is_dma)
        oid = len(self.ops)
        for x in reads:
            b = x.buf if isinstance(x, V) else x
            if b is None or not isinstance(b, Buf):
                continue
            if b.w is not None:
                op.deps.add(b.w)
        for x in writes:
            b = x.buf if isinstance(x, V) else x
            if b is None or not isinstance(b, Buf):
                continue
            if b.w is not None:
                op.deps.add(b.w)
            for r in b.r:
                op.deps.add(r)
        op.deps.discard(oid)
        for x in reads:
            b = x.buf if isinstance(x, V) else x
            if isinstance(b, Buf):
                b.r.append(oid)
        for x in writes:
            b = x.buf if isinstance(x, V) else x
            if isinstance(b, Buf):
                b.w = oid
                b.r = []
        self.ops.append(op)
        self.eng_ops[eng].append(oid)
        if is_dma:
            self.dma_since.append(oid)
        elif not getattr(fn, "_waitonly", False):
            self.last_real[eng] = oid
        return oid

    def barrier(self):
        deps = set(self.last_real.values()) | set(self.dma_since)
        self.dma_since = []
        for eng in self.eng_ops:
            fn = lambda e: None
            op = Op(eng, fn, False)
            op.deps = set(deps)
            self.ops.append(op)
            self.eng_ops[eng].append(len(self.ops) - 1)

    def finalize(self, stack):
        nc = self.nc
        ops = self.ops
        esem = {e: stack.enter_context(nc.semaphore(f"s_{e}")) for e in self.eng_ops}
        dsems = {e: [stack.enter_context(nc.semaphore(f"d_{e}_{i}")) for i in range(self.n_dma_sems)]
                 for e in ("sync", "gpsimd", "scalar")}
        dcount = {e: [0] * self.n_dma_sems for e in dsems}
        drot = {e: 0 for e in dsems}
        prev_on_dsem = {}
        ccsem = stack.enter_context(nc.semaphore("s_cc"))
        cccount = 0
        for oid, op in enumerate(ops):
            if op.is_cc:
                cccount += 1
                op.dsem = ccsem
                op.semval = cccount
                op.needed = True
                continue
            if op.is_dma:
                k = drot[op.eng]
                drot[op.eng] = (k + 1) % self.n_dma_sems
                dcount[op.eng][k] += 16
                op.dsem = dsems[op.eng][k]
                op.semval = dcount[op.eng][k]
                key = (op.eng, k)
                if key in prev_on_dsem:
                    op.deps.add(prev_on_dsem[key])
                prev_on_dsem[key] = oid
                op.needed = True
        for oid, op in enumerate(ops):
            nd = set()
            best = {}
            for d in op.deps:
                dop = ops[d]
                if dop.fn is None:
                    continue
                if dop.is_dma:
                    nd.add(d)
                    continue
                if (not op.is_dma) and dop.eng == op.eng == "tensor":
                    continue
                if best.get(dop.eng, -1) < d:
                    best[dop.eng] = d
            nd |= set(best.values())
            op.deps = nd
            for d in nd:
                ops[d].needed = True
        ecount = {e: 0 for e in self.eng_ops}
        for oid, op in enumerate(ops):
            if (not op.is_dma) and op.needed:
                ecount[op.eng] += 1
                op.dsem = esem[op.eng]
                op.semval = ecount[op.eng]
        self.sem_counts = dict(ecount)
        self.n_wait = 0
        rec = self

        def replay(e, ename):
            seen = {}
            for oid in rec.eng_ops[ename]:
                op = ops[oid]
                need = {}
                for d in op.deps:
                    dop = ops[d]
                    s = dop.dsem
                    key = id(s)
                    if seen.get(key, 0) >= dop.semval:
                        continue
                    if key not in need or need[key][1] < dop.semval:
                        need[key] = (s, dop.semval)
                for key, (s, val) in need.items():
                    e.wait_ge(s, val)
                    seen[key] = val
                    rec.n_wait += 1
                ins = op.fn(e)
                if op.needed and ins is not None:
                    if op.is_cc:
                        ins.then_inc(op.dsem)
                    else:
                        ins.then_inc(op.dsem, 16 if op.is_dma else 1)

        block = stack.enter_context(nc.Block())

        @block.sync
        def _(e):
            replay(e, "sync")

        @block.tensor
        def _(e):
            replay(e, "tensor")

        @block.vector
        def _(e):
            replay(e, "vector")

        @block.scalar
        def _(e):
            replay(e, "scalar")

        @block.gpsimd
        def _(e):
            replay(e, "gpsimd")

    def dma(self, out, in_, q="sync", **kw):
        o, i = _ap(out), _ap(in_)
        return self.emit(q, lambda e: e.dma_start(out=o, in_=i, **kw), reads=[in_], writes=[out], is_dma=True)

    def mm(self, out, lhsT, rhs, start=True, stop=True, **kw):
        o, l, r = _ap(out), _ap(lhsT), _ap(rhs)
        return self.emit("tensor", lambda e: e.matmul(o, l, r, start=start, stop=stop, **kw),
                         reads=[lhsT, rhs], writes=[out])

    def transpose(self, out, in_, ident):
        o, i, d = _ap(out), _ap(in_), _ap(ident)
        return self.emit("tensor", lambda e: e.transpose(o, i, d), reads=[in_, ident], writes=[out])

    def act(self, out, in_, func, bias=None, scale=None, accum_out=None, eng="scalar"):
        o, i = _ap(out), _ap(in_)
        kw = {}
        reads = [in_]
        writes = [out]
        if bias is not None:
            kw["bias"] = _ap(bias)
            reads.append(bias)
        if scale is not None:
            kw["scale"] = _ap(scale)
            reads.append(scale)
        if accum_out is not None:
            kw["accum_out"] = _ap(accum_out)
            writes.append(accum_out)
        return self.emit("scalar", lambda e: e.activation(o, i, func, **kw), reads=reads, writes=writes)

    def ts(self, out, in0, s1, s2, op0, op1=None, accum_out=None, eng="vector"):
        o, i = _ap(out), _ap(in0)
        a1, a2 = _ap(s1), _ap(s2)
        reads = [in0, s1, s2]
        writes = [out]
        kw = {}
        if op1 is not None:
            kw["op1"] = op1
        if accum_out is not None:
            kw["accum_out"] = _ap(accum_out)
            writes.append(accum_out)
        return self.emit(eng, lambda e: e.tensor_scalar(o, i, a1, a2, op0, **kw), reads=reads, writes=writes)

    def tt(self, out, in0, in1, op, eng="vector"):
        o, a, b = _ap(out), _ap(in0), _ap(in1)
        return self.emit(eng, lambda e: e.tensor_tensor(o, a, b, op), reads=[in0, in1], writes=[out])

    def stt(self, out, in0, scalar, in1, op0, op1, eng="vector"):
        o, a, s, b = _ap(out), _ap(in0), _ap(scalar), _ap(in1)
        return self.emit(eng, lambda e: e.scalar_tensor_tensor(o, a, s, b, op0, op1),
                         reads=[in0, scalar, in1], writes=[out])

    def copy(self, out, in_, eng="vector"):
        o, i = _ap(out), _ap(in_)
        if eng == "scalar":
            return self.emit(eng, lambda e: e.copy(o, i), reads=[in_], writes=[out])
        return self.emit(eng, lambda e: e.tensor_copy(o, i), reads=[in_], writes=[out])

    def memset(self, out, val, eng="vector"):
        o = _ap(out)
        return self.emit(eng, lambda e: e.memset(o, val), writes=[out])

    def recip(self, out, in_):
        o, i = _ap(out), _ap(in_)
        return self.emit("vector", lambda e: e.reciprocal(o, i), reads=[in_], writes=[out])

    def reduce(self, out, in_, op, axis=mybir.AxisListType.X, eng="vector"):
        o, i = _ap(out), _ap(in_)
        return self.emit(eng, lambda e: e.tensor_reduce(o, i, axis, op), reads=[in_], writes=[out])

    def scan(self, out, d0, d1, initial, op0, op1):
        o, a, b, ini = _ap(out), _ap(d0), _ap(d1), _ap(initial)
        return self.emit("vector", lambda e: e.tensor_tensor_scan(o, a, b, ini, op0, op1),
                         reads=[d0, d1, initial], writes=[out])

    def cc(self, fn):
        oid = self.emit("gpsimd", fn, is_dma=True)
        self.ops[oid].is_cc = True
        return oid

    def wait_all(self, eng, bufs):
        return self.emit(eng, lambda e: None, reads=list(bufs), writes=[])

    def generic(self, eng, fn, reads=(), writes=()):
        return self.emit(eng, fn, reads=reads, writes=writes)

class Cfg:
    def __init__(s, D=2048, S=4096, QB=512, HF=6, HDS=6, HM=4, IH=16, ID=64, TOPK=256, NM=256, NIT=20):
        s.D, s.S, s.QB, s.HF, s.HDS, s.HM, s.IH, s.ID, s.TOPK, s.NM, s.NIT = D, S, QB, HF, HDS, HM, IH, ID, TOPK, NM, NIT
        s.HD = 128
        s.DF, s.DD, s.DM = HF * 128, HDS * 128, HM * 128
        s.DMIX = s.DF + s.DD + s.DM
        sp = (s.DF, s.DF, s.DF, s.DF, HF, s.DD, s.DD, s.DD, s.DD, IH * ID, ID, IH, s.DM, s.DM)
        names = ("fq", "fk", "fv", "fg", "fl", "dq", "dk", "dv", "dg", "iq", "ik", "iw", "mq", "mg")
        off = 0
        s.col = {}
        for n, w in zip(names, sp):
            s.col[n] = (off, w)
            off += w
        s.NIN = off
        s.KC = D // 128
        s.So = S // 2
        assert S == 8 * QB and QB % 128 == 0 and QB <= 512
        s.TPB = QB // 128
        s.NT = S // 128
        s.NTo = s.So // 128
        s.NBLK = s.DMIX // 128
        s.SCALE = 128 ** -0.5
        s.WSCALE = (IH ** -0.5) * (ID ** -0.5)
        s.ALPHA = 8 ** 0.25
        s.NCH = max(1, D // 512)
        s.CW = min(512, D)
        assert IH % 2 == 0 and ID == 64


OWN = {0: [0, 3, 4, 7], 1: [1, 2, 5, 6]}
NEGB = -30000.0


def build_program(C, L=4, groups=None, LW=None):
    LW = LW or L
    nc = bass.Bass("TRN2", target_bir_lowering=False)
    D, S, QB, So, KC, NT, NTo, TPB = C.D, C.S, C.QB, C.So, C.KC, C.NT, C.NTo, C.TPB
    HF, HDS, HM, IH = C.HF, C.HDS, C.HM, C.IH

    def din(name, shape, dt=F32):
        return nc.dram_tensor(name, list(shape), dt, kind="ExternalInput").ap()

    def dscr(name, shape, dt):
        return nc.dram_tensor(name, list(shape), dt, kind="Internal").ap()

    xT = din("xT", [D, S])
    x_own0 = din("x_own", [So, D])
    memT = din("memT", [D, C.NM])
    w_in_all = [din(f"w_in{i}", [D, C.NIN]) for i in range(LW)]
    w_kv_all = [din(f"w_kv{i}", [D, 2 * C.DM]) for i in range(LW)]
    w_out_all = [din(f"w_out{i}", [C.DMIX, D]) for i in range(LW)]
    negbf_all = din("negbf", [LW, HF, 1])
    ln_g_all = din("ln_g", [LW, 1, D])
    ln_b_all = din("ln_b", [LW, 1, D])
    sel_d = din("sel", [128, 2])
    if groups is None:
        groups = [[0, 1], [2, 3], [4, 5], [6, 7]]
    cos128 = din("cos128", [128, S])
    sin128 = din("sin128", [128, S])
    cos64 = din("cos64", [128, S])
    sin64 = din("sin64", [128, S])
    rot128 = din("rot128", [128, 128])
    rot64 = din("rot64", [128, 128])
    ident_d = din("ident", [128, 128])
    posq_d = din("posq", [128, NTo])
    posk_d = din("posk", [1, S])
    Mb_d = din("Mb", [HF, 8, 8])
    pmask_d = din("pmask", [128, 4 * QB])
    dmask_d = din("dmask", [128, TPB * QB])
    out_d = nc.dram_tensor("out", [So, D], F32, kind="ExternalOutput").ap()

    QfT = dscr("QfT", [HF, 128, So], BF16)
    KfT = dscr("KfT", [HF, 128, S], BF16)
    QdT = dscr("QdT", [HDS, 128, So], BF16)
    KdT = dscr("KdT", [HDS, 128, S], BF16)
    QiT = dscr("QiT", [IH // 2, 128, So], BF16)
    KiTd = dscr("KiT", [128, S], BF16)
    QmT = dscr("QmT", [HM, 128, So], BF16)
    GT = dscr("GT", [C.NBLK, 128, So], F32)
    Vf = dscr("Vf", [S, C.DF], BF16)
    Vd = dscr("Vd", [S, C.DD], BF16)
    CnD = dscr("CnD", [HF, S], F32)
    YT = dscr("YT", [C.NBLK, 128, So], BF16)
    LFD = dscr("LFD", [HF, S], F32)
    CBD = dscr("CBD", [3, HF, So], BF16)
    dbgM = dscr("dbgM", [So, S], BF16) if getattr(C, "DBG", False) else None
    xcur = dscr("xcur", [So, D], F32)
    xTo = dscr("xTo", [D, So], BF16)
    XCH = min(getattr(C, "XCH", 512), D)
    NXC = D // XCH
    Gx = dscr("Gx", [NXC, 2, XCH, So], BF16)

    with ExitStack() as st:
        R = Rec(nc)

        uid = [0]

        def SB(stk, name, shape, dt):
            uid[0] += 1
            t = stk.enter_context(nc.sbuf_tensor(f"{name}_{uid[0]}", list(shape), dt))
            return Buf(t.ap(), name)

        def PS(stk, name, shape, dt):
            uid[0] += 1
            t = stk.enter_context(nc.psum_tensor(f"{name}_{uid[0]}", list(shape), dt))
            return Buf(t.ap(), name)

        class Pool:
            def __init__(s, bufs):
                s.bufs, s.i = bufs, 0

            def next(s):
                b = s.bufs[s.i % len(s.bufs)]
                s.i += 1
                return b

        def mkpool(stk, name, n, shape, dt, ps=False):
            return Pool([(PS if ps else SB)(stk, f"{name}{i}", shape, dt) for i in range(n)])

        identb = SB(st, "identb", [128, 128], BF16)
        rot128b = SB(st, "rot128b", [128, 128], BF16)
        rot64b = SB(st, "rot64b", [128, 128], BF16)
        ones128 = SB(st, "ones128", [128, 128], BF16)
        pmaskb = SB(st, "pmaskb", [128, 4 * QB], BF16)
        dmaskb = SB(st, "dmaskb", [128, TPB * QB], BF16)
        POSQ = SB(st, "POSQ", [128, NTo], F32)
        WI = SB(st, "WI", [128, NTo, IH], F32)
        NEGBF = SB(st, "NEGBF", [HF, 1], F32)
        KmT = SB(st, "KmT", [128, HM, C.NM], BF16)
        Vm = SB(st, "Vm", [128, C.NM // 128, C.DM], BF16)

        R.dma(identb.v, ident_d, q="gpsimd")
        R.dma(rot128b.v, rot128, q="gpsimd")
        R.dma(rot64b.v, rot64, q="gpsimd")
        R.dma(pmaskb.v, pmask_d, q="gpsimd")
        R.dma(dmaskb.v, dmask_d, q="gpsimd")
        R.dma(POSQ.v, posq_d)
        SEL = SB(st, "SEL", [128, 2], F32)
        R.dma(SEL.v, sel_d)
        R.memset(ones128.v, 1.0)

        for l in range(L):
            w_in, w_kv, w_out = w_in_all[l % LW], w_kv_all[l % LW], w_out_all[l % LW]
            ln_g, ln_b = ln_g_all[l % LW], ln_b_all[l % LW]
            x_own = x_own0 if l == 0 else xcur
            last = (l == L - 1)
            R.dma(NEGBF.v, negbf_all[l % LW])
            with ExitStack() as ph:
                XT = SB(ph, "XT", [128, KC, S], BF16)
                wpool = mkpool(ph, "Wt", 2, [128, KC, 512], BF16)
                psA = mkpool(ph, "psA", 4, [128, 512], F32, ps=True)
                psR = mkpool(ph, "psR", 2, [128, 512], F32, ps=True)
                stb = mkpool(ph, "stb", 3, [128, 512], BF16)
                stf = mkpool(ph, "stf", 2, [128, 512], F32)
                tmpb = mkpool(ph, "tmpb", 2, [128, 512], BF16)
                tmpa = mkpool(ph, "tmpa", 1, [128, 512], F32)
                tmpc = mkpool(ph, "tmpc", 1, [128, 512], F32)
                cst = mkpool(ph, "cst", 2, [128, 512], F32)
                snt = mkpool(ph, "snt", 2, [128, 512], F32)
                evac_i = [0]
                if l == 0:
                    for kc in range(KC):
                        R.dma(XT[:, kc, :], xT[kc * 128:(kc + 1) * 128, :], q="gpsimd")
                else:
                    for kc in range(KC):
                        R.dma(XT[:, kc, 0:So], xTo[kc * 128:(kc + 1) * 128, :])
                        for tb in range(So // QB):
                            g0, g1, tm_ = stb.next(), stb.next(), tmpb.next()
                            xq, xr = (kc * 128) // XCH, (kc * 128) % XCH
                            R.dma(g0[:, :QB], Gx[xq, 0, xr:xr + 128, tb * QB:(tb + 1) * QB])
                            R.dma(g1[:, :QB], Gx[xq, 1, xr:xr + 128, tb * QB:(tb + 1) * QB])
                            R.ts(tm_[:, :QB], g0[:, :QB], SEL[:, 0:1], None, ALU.mult)
                            R.stt(XT[:, kc, So + tb * QB:So + (tb + 1) * QB], g1[:, :QB], SEL[:, 1:2], tm_[:, :QB],
                                  ALU.mult, ALU.add)

                def load_w(src, c0, w, dup=False):
                    Wt = wpool.next()
                    srcv = src[:, c0:c0 + w].rearrange("(kc p) n -> p kc n", p=128)
                    R.dma(Wt[:, :, 0:w], srcv, q="gpsimd")
                    if dup:
                        R.dma(Wt[:, :, w:2 * w], srcv, q="gpsimd")
                    return Wt

                def tform(Wt, coff, M, src_xt, tb, width):
                    ps = psA.next()
                    for kc in range(KC):
                        R.mm(ps[:M, :width], Wt[:, kc, coff:coff + M], src_xt[:, kc, tb * width:(tb + 1) * width],
                             start=(kc == 0), stop=(kc == KC - 1))
                    return ps

                def evac_bf(dst, src, scale=None):
                    evac_i[0] += 1
                    if scale is None and evac_i[0] % 2 == 0:
                        R.copy(dst, src, eng="vector")
                    else:
                        R.act(dst, src, AF.Identity, scale=(1.0 if scale is None else scale))

                def rope_post(ps, tb, scale, rotb, cosd, sind, dst):
                    t0 = tmpb.next()
                    R.act(t0[:, :QB], ps[:, :QB], AF.Identity, scale=scale)
                    pr = psR.next()
                    R.mm(pr[:, :QB], rotb.v, t0[:, :QB])
                    c = cst.next()
                    sn = snt.next()
                    R.dma(c[:, :QB], cosd[:, tb * QB:(tb + 1) * QB])
                    R.dma(sn[:, :QB], sind[:, tb * QB:(tb + 1) * QB])
                    a = tmpa.next()
                    b = tmpc.next()
                    R.tt(a[:, :QB], t0[:, :QB], c[:, :QB], ALU.mult)
                    R.tt(b[:, :QB], pr[:, :QB], sn[:, :QB], ALU.mult)
                    o = stb.next()
                    R.tt(o[:, :QB], a[:, :QB], b[:, :QB], ALU.add)
                    R.dma(dst, o[:, :QB])

                def seg_T(name, kind, dst, ntb, nblocks=None):
                    c0, w = C.col[name]
                    nb = w // 128
                    for s0 in range(0, nb, 4):
                        nbl = min(4, nb - s0)
                        Wt = load_w(w_in, c0 + s0 * 128, nbl * 128)
                        for bl in range(nbl):
                            blk = s0 + bl
                            for tb in range(ntb):
                                ps = tform(Wt, bl * 128, 128, XT, tb, QB)
                                d = dst(blk, tb)
                                if kind == "q":
                                    o = stb.next()
                                    R.act(o[:, :QB], ps[:, :QB], AF.Identity, scale=C.SCALE)
                                    R.dma(d, o[:, :QB])
                                elif kind == "k":
                                    o = stb.next()
                                    evac_bf(o[:, :QB], ps[:, :QB])
                                    R.dma(d, o[:, :QB])
                                elif kind == "gate":
                                    o = stf.next()
                                    R.act(o[:, :QB], ps[:, :QB], AF.Silu)
                                    R.dma(d, o[:, :QB])
                                elif kind == "qr":
                                    rope_post(ps, tb, C.SCALE, rot128b, cos128, sin128, d)
                                elif kind == "kr":
                                    rope_post(ps, tb, 1.0, rot128b, cos128, sin128, d)
                                elif kind == "iq":
                                    rope_post(ps, tb, 1.0, rot64b, cos64, sin64, d)

                def seg_N(name, dst_d):
                    c0, w = C.col[name]
                    for s0 in range(0, w, 512):
                        ww = min(512, w - s0)
                        Wt = load_w(w_in, c0 + s0, ww)
                        for t in range(NT):
                            ps = psA.next()
                            for kc in range(KC):
                                R.mm(ps[:, :ww], XT[:, kc, t * 128:(t + 1) * 128], Wt[:, kc, 0:ww],
                                     start=(kc == 0), stop=(kc == KC - 1))
                            o = stb.next()
                            evac_bf(o[:, :ww], ps[:, :ww])
                            R.dma(dst_d[t * 128:(t + 1) * 128, s0:s0 + ww], o[:, :ww])

                NO = So // QB
                NA = S // QB
                sl = lambda tb: slice(tb * QB, (tb + 1) * QB)
                seg_T("fq", "q", lambda b, tb: QfT[b, :, sl(tb)], NO)
                seg_T("fk", "k", lambda b, tb: KfT[b, :, sl(tb)], NA)
                seg_N("fv", Vf)
                seg_T("fg", "gate", lambda b, tb: GT[b, :, sl(tb)], NO)
                c0, w = C.col["fl"]
                Wt = load_w(w_in, c0, HF)
                for tb in range(NA):
                    ps = tform(Wt, 0, HF, XT, tb, QB)
                    e = stf.next()
                    R.act(e[:HF, :QB], ps[:HF, :QB], AF.Exp, scale=-1.0, bias=NEGBF.v)
                    e2 = stf.next()
                    R.act(e2[:HF, :QB], e[:HF, :QB], AF.Ln, bias=1.0)
                    R.dma(LFD[:, sl(tb)], e2[:HF, :QB])
                seg_T("dq", "qr", lambda b, tb: QdT[b, :, sl(tb)], NO)
                seg_T("dk", "kr", lambda b, tb: KdT[b, :, sl(tb)], NA)
                seg_N("dv", Vd)
                seg_T("dg", "gate", lambda b, tb: GT[HF + b, :, sl(tb)], NO)
                seg_T("iq", "iq", lambda b, tb: QiT[b, :, sl(tb)], NO)
                c0, w = C.col["ik"]
                Wt = load_w(w_in, c0, 64, dup=True)
                for tb in range(NA):
                    ps = tform(Wt, 0, 128, XT, tb, QB)
                    rope_post(ps, tb, 1.0, rot64b, cos64, sin64, KiTd[:, sl(tb)])
                c0, w = C.col["iw"]
                Wt = load_w(w_in, c0, IH)
                for t in range(NTo):
                    ps = psA.next()
                    for kc in range(KC):
                        R.mm(ps[:, :IH], XT[:, kc, t * 128:(t + 1) * 128], Wt[:, kc, 0:IH],
                             start=(kc == 0), stop=(kc == KC - 1))
                    R.act(WI[:, t, :], ps[:, :IH], AF.Identity, scale=C.WSCALE)
                seg_T("mq", "q", lambda b, tb: QmT[b, :, sl(tb)], NO)
                seg_T("mg", "gate", lambda b, tb: GT[HF + HDS + b, :, sl(tb)], NO)
            R.barrier()
            with ExitStack() as ph:
                MT = SB(ph, "MT", [128, KC, C.NM], BF16)
                for kc in range(KC):
                    R.dma(MT[:, kc, :], memT[kc * 128:(kc + 1) * 128, :], q="gpsimd")
                wpool = mkpool(ph, "Wm", 2, [128, KC, 512], BF16)
                psA = mkpool(ph, "psM", 4, [128, 512], F32, ps=True)
                evac_i = [0]

                def load_w(src, c0, w):
                    Wt = wpool.next()
                    R.dma(Wt[:, :, 0:w], src[:, c0:c0 + w].rearrange("(kc p) n -> p kc n", p=128), q="gpsimd")
                    return Wt

                def tform(Wt, coff, M, src_xt, tb, width):
                    ps = psA.next()
                    for kc in range(KC):
                        R.mm(ps[:M, :width], Wt[:, kc, coff:coff + M], src_xt[:, kc, tb * width:(tb + 1) * width],
                             start=(kc == 0), stop=(kc == KC - 1))
                    return ps

                def evac_bf(dst, src):
                    evac_i[0] += 1
                    if evac_i[0] % 2 == 0:
                        R.copy(dst, src, eng="vector")
                    else:
                        R.act(dst, src, AF.Identity, scale=1.0)
                for s0 in range(0, HM, 4):
                    nbl = min(4, HM - s0)
                    Wt = load_w(w_kv, s0 * 128, nbl * 128)
                    for bl in range(nbl):
                        ps = tform(Wt, bl * 128, 128, MT, 0, C.NM)
                        evac_bf(KmT[:, s0 + bl, :], ps[:, :C.NM])
                for s0 in range(0, C.DM, 512):
                    ww = min(512, C.DM - s0)
                    Wt = load_w(w_kv, C.DM + s0, ww)
                    for t in range(C.NM // 128):
                        ps = psA.next()
                        for kc in range(KC):
                            R.mm(ps[:, :ww], MT[:, kc, t * 128:(t + 1) * 128], Wt[:, kc, 0:ww],
                                 start=(kc == 0), stop=(kc == KC - 1))
                        evac_bf(Vm[:, t, s0:s0 + ww], ps[:, :ww])
            R.barrier()

            with ExitStack() as ph:
                onesf = SB(ph, "onesf", [HF, QB], F32)
                Cw = SB(ph, "Cw", [HF, S], F32)
                Cn = SB(ph, "Cn", [HF, S], F32)
                Mb = SB(ph, "Mb", [HF, 8, 8], F32)
                tmp88 = SB(ph, "tmp88", [HF, 8, 8], F32)
                tot = SB(ph, "tot", [HF, 8], F32)
                off = SB(ph, "off", [HF, 8], F32)
                LF = SB(ph, "LF", [HF, S], F32)
                R.dma(LF.v, LFD)
                R.memset(onesf.v, 1.0)
                R.dma(Mb.v, Mb_d)
                for b in range(8):
                    R.scan(Cw[:, b * QB:(b + 1) * QB], onesf.v, LF[:, b * QB:(b + 1) * QB], 0.0, ALU.mult, ALU.add)
                Cw3 = Cw.v.re("h (b q) -> h b q", q=QB)
                R.copy(tot.v, Cw3[:, :, QB - 1])
                totb = V(tot.ap.unsqueeze(1).to_broadcast([HF, 8, 8]), tot)
                R.tt(tmp88.v, Mb.v, totb, ALU.mult)
                R.reduce(off.v, tmp88.v, ALU.add)
                offb = V(off.ap.unsqueeze(2).to_broadcast([HF, 8, QB]), off)
                R.tt(Cn.v.re("h (b q) -> h b q", q=QB), Cw3, offb, ALU.add)
                CnDb = Buf(CnD, "CnD")
                R.dma(CnDb.v, Cn.v)
                rowp = mkpool(ph, "row", 2, [1, So], F32)
                r1p = mkpool(ph, "r1", 2, [1, So], F32)
                hip = mkpool(ph, "hi", 2, [1, So], BF16)
                midp = mkpool(ph, "mid", 2, [1, So], BF16)
                lop = mkpool(ph, "lo", 2, [1, So], BF16)
                for h in range(HF):
                    row, r1, hi, mid, lo = rowp.next(), r1p.next(), hip.next(), midp.next(), lop.next()
                    R.dma(row.v, CnDb[h:h + 1, 0:So])
                    R.ts(hi.v, row.v, -1.0, None, ALU.mult)
                    R.stt(r1.v, row.v, -1.0, hi.v, ALU.mult, ALU.subtract)
                    R.copy(mid.v, r1.v)
                    R.tt(r1.v, r1.v, mid.v, ALU.subtract)
                    R.copy(lo.v, r1.v)
                    R.dma(CBD[0, h:h + 1, :], hi.v)
                    R.dma(CBD[1, h:h + 1, :], mid.v)
                    R.dma(CBD[2, h:h + 1, :], lo.v)
            R.barrier()

            def attn_unit(pools, QTh, qcol0, tiles, cb, gate_src, out_dst):
                Sp, Pp, Op, Up, misc = pools
                Ops, Ups = Op.next(), Up.next()
                n = len(tiles)

                def qk(i):
                    kt, vt, mk, bs = tiles[i]
                    sp = Sp.next()
                    last_qk = (cb is None and mk is None)
                    R.mm(sp[:, :QB], kt, QTh[:, qcol0:qcol0 + QB], start=True, stop=last_qk)
                    if cb is not None:
                        R.mm(sp[:, :QB], ones128[0:3, :], cb, start=False, stop=(mk is None))
                    if mk is not None:
                        R.mm(sp[:, :QB], identb.v, mk, start=False, stop=True)
                    return sp

                sps = {0: qk(0)}
                for i in range(n):
                    if i + 1 < n:
                        sps[i + 1] = qk(i + 1)
                    kt, vt, mk, bs = tiles[i]
                    p = Pp.next()
                    sp = sps.pop(i)
                    if bs is not None:
                        R.act(p[:, :QB], sp[:, :QB], AF.Exp, bias=bs)
                    else:
                        R.act(p[:, :QB], sp[:, :QB], AF.Exp)
                    R.mm(Ops[:, :QB], vt, p[:, :QB], start=(i == 0), stop=(i == n - 1))
                    R.mm(Ups[:, :QB], ones128.v, p[:, :QB], start=(i == 0), stop=(i == n - 1))
                rs, y1, g, yb = [m.next() for m in misc]
                R.dma(g[:, :QB], gate_src)
                R.recip(rs[:, :QB], Ups[:, :QB])
                R.tt(y1[:, :QB], Ops[:, :QB], rs[:, :QB], ALU.mult)
                R.tt(yb[:, :QB], y1[:, :QB], g[:, :QB], ALU.mult)
                R.dma(out_dst, yb[:, :QB])

            def attn_pools(ph, nO=2):
                Sp = mkpool(ph, "Sps", 2, [128, 512], F32, ps=True)
                Op = mkpool(ph, "Ops", nO, [128, 512], F32, ps=True)
                Up = mkpool(ph, "Ups", nO, [128, 512], F32, ps=True)
                Pp = mkpool(ph, "Pt", 3, [128, 512], BF16)
                misc = [mkpool(ph, "rs", 2, [128, 512], F32), mkpool(ph, "y1", 2, [128, 512], F32),
                        mkpool(ph, "gt", 2, [128, 512], F32), mkpool(ph, "yb", 2, [128, 512], BF16)]
                return (Sp, Pp, Op, Up, misc)

            def local_tiles(j):
                own = list(range(0, (j + 1) * TPB))
                par = list(range(4 * TPB, (5 + j) * TPB))
                return own + par

            with ExitStack() as ph:
                pools = attn_pools(ph)
                ktp = mkpool(ph, "KTh", 2, [128, S], BF16)
                vtp = mkpool(ph, "Vh", 2, [128, NT, 128], BF16)
                qtp = mkpool(ph, "QTh", 2, [128, So], BF16)
                CS = SB(ph, "CS", [128, NT, HF], F32)
                CB = SB(ph, "CB", [3, HF, So], BF16)
                for kt in range(NT):
                    R.dma(CS[:, kt, :], CnD[:, kt * 128:(kt + 1) * 128].rearrange("h p -> p h"),
                          allow_slow_non_contiguous=True)
                R.dma(CB.v, CBD)
                for h in range(HF):
                    KTh, Vh, QTh = ktp.next(), vtp.next(), qtp.next()
                    R.dma(KTh.v, KfT[h])
                    R.dma(Vh.v, Vf[:, h * 128:(h + 1) * 128].rearrange("(kt p) d -> p kt d", p=128))
                    R.dma(QTh.v, QfT[h])
                    for j in range(4):
                        tiles = []
                        lt = local_tiles(j)
                        for i, kt in enumerate(lt):
                            mk = None
                            if j * TPB <= i < (j + 1) * TPB:
                                mk = dmaskb[:, (i - j * TPB) * QB:(i - j * TPB + 1) * QB]
                            elif i >= (2 * j + 1) * TPB:
                                mk = pmaskb[:, j * QB:(j + 1) * QB]
                            tiles.append((KTh[:, kt * 128:(kt + 1) * 128], Vh[:, kt, :], mk, CS[:, kt, h:h + 1]))
                        attn_unit(pools, QTh, j * QB, tiles, CB[0:3, h, j * QB:(j + 1) * QB],
                                  GT[h, :, j * QB:(j + 1) * QB], YT[h, :, j * QB:(j + 1) * QB])
            R.barrier()

            with ExitStack() as ph:
                pools = attn_pools(ph, nO=1)
                zp = mkpool(ph, "zps", 2, [128, 512], F32, ps=True)
                accp = mkpool(ph, "accps", 1, [128, 512], F32, ps=True)
                trp = mkpool(ph, "trp", 1, [128, 4, 128], BF16, ps=True)
                ktp = mkpool(ph, "KTd", 1, [128, S], BF16)
                vtp = mkpool(ph, "Vdh", 1, [128, NT, 128], BF16)
                qtp = mkpool(ph, "QTd", 2, [128, QB], BF16)
                KiT = SB(ph, "KiTs", [128, S], BF16)
                poskp = mkpool(ph, "POSK", 2, [128, 2, QB], F32)
                Ip = mkpool(ph, "Ib", 4, [128, S], F32)
                NEGMp = mkpool(ph, "NEGM", 2, [128, S], BF16)
                NEGMT = SB(ph, "NEGMT", [128, NT, QB], BF16)
                rp = mkpool(ph, "relu", 3, [128, 512], BF16)
                tmpm = mkpool(ph, "tmpm", 2, [128, 512], F32)
                qip = mkpool(ph, "QiTt", 2, [128, IH // 2, 128], BF16)
                dgp = mkpool(ph, "DG", 2, [128, IH, 128], BF16)
                sc = {n: mkpool(ph, n, 4, [128, 1], F32) for n in ("mx", "mn", "lo", "w0", "mid", "cnt")}
                predp = mkpool(ph, "pred", 4, [128, 1], mybir.dt.uint32)
                R.dma(KiT.v, KiTd)
                POSKs = {}

                def idx_tile(j, it):
                    tt_ = j * TPB + it
                    nch = 2 * (j + 1)
                    Ib, Qi, DG = Ip.next(), qip.next(), dgp.next()
                    R.dma(Qi.v, QiT[:, :, tt_ * 128:(tt_ + 1) * 128].rearrange("g p s -> p g s"))
                    idb = V(identb.ap.unsqueeze(1).to_broadcast([128, IH, 128]), identb)
                    wib = V(WI.ap[:, tt_, :].unsqueeze(2).to_broadcast([128, IH, 128]), WI)
                    R.tt(DG.v, idb, wib, ALU.mult, eng="gpsimd")
                    for c in range(nch):
                        kcol0 = c * QB if c < j + 1 else 4 * QB + (c - (j + 1)) * QB
                        acc = accp.next()

                        def zmm(h):
                            z = zp.next()
                            po = (h % 2) * 64
                            R.mm(z[:, :QB], Qi[po:po + 64, h // 2, :], KiT[po:po + 64, kcol0:kcol0 + QB])
                            return z

                        zs = {0: zmm(0)}
                        for h in range(IH):
                            if h + 1 < IH:
                                zs[h + 1] = zmm(h + 1)
                            r = rp.next()
                            R.act(r[:, :QB], zs.pop(h)[:, :QB], AF.Relu)
                            R.mm(acc[:, :QB], DG[:, h, :], r[:, :QB], start=(h == 0), stop=(h == IH - 1))
                        R.copy(Ib[:, c * QB:(c + 1) * QB], acc[:, :QB], eng="vector")
                    return dict(j=j, it=it, tt=tt_, Ib=Ib, ncols=nch * QB, nk=nch * TPB)

                def bis_group(tiles):
                    for T_ in tiles:
                        j, tt_, Ib, ncols = T_["j"], T_["tt"], T_["Ib"], T_["ncols"]
                        for n_ in ("mx", "mn", "lo", "w0", "mid", "cnt"):
                            T_[n_] = sc[n_].next()
                        T_["NEGM"] = NEGMp.next()
                        R.reduce(T_["mx"].v, Ib[:, :ncols], ALU.max)
                        R.reduce(T_["mn"].v, Ib[:, :ncols], ALU.min)
                        POSK = POSKs[j]
                        for c, pi in ((j, 0), (2 * j + 1, 1)):
                            tm = tmpm.next()
                            R.ts(tm[:, :QB], POSK[:, pi, :], POSQ[:, tt_:tt_ + 1], -2e30, ALU.is_gt, op1=ALU.mult)
                            R.tt(Ib[:, c * QB:(c + 1) * QB], Ib[:, c * QB:(c + 1) * QB], tm[:, :QB], ALU.add)
                        R.ts(T_["lo"].v, T_["mn"].v, -1.0, None, ALU.add)
                        R.ts(T_["w0"].v, T_["mx"].v, 1.0, T_["lo"].v, ALU.add, op1=ALU.subtract)
                    for k in range(C.NIT):
                        for T_ in tiles:
                            R.ts(T_["mid"].v, T_["w0"].v, 0.5 ** (k + 1), T_["lo"].v, ALU.mult, op1=ALU.add)
                        for T_ in tiles:
                            R.ts(T_["NEGM"][:, :T_["ncols"]], T_["Ib"][:, :T_["ncols"]], T_["mid"].v, None, ALU.is_ge,
                                 op1=ALU.add, accum_out=T_["cnt"].v)
                        for T_ in tiles:
                            pred = predp.next()
                            T_["pred"] = pred
                            R.ts(pred.v, T_["cnt"].v, float(min(C.TOPK, S // 4)), None, ALU.is_ge)
                        for T_ in tiles:
                            lo_ap, pr_ap, mid_ap = T_["lo"].ap, T_["pred"].ap, T_["mid"].ap
                            R.generic("vector", lambda e, a=lo_ap, b=pr_ap, c_=mid_ap: e.copy_predicated(a, b, c_),
                                      reads=[T_["pred"].v, T_["mid"].v, T_["lo"].v], writes=[T_["lo"].v])
                    for T_ in tiles:
                        ncols, nk, it, tt_ = T_["ncols"], T_["nk"], T_["it"], T_["tt"]
                        NEGM = T_["NEGM"]
                        R.ts(NEGM[:, :ncols], T_["Ib"][:, :ncols], T_["lo"].v, NEGB, ALU.is_lt, op1=ALU.mult)
                        if dbgM is not None:
                            R.dma(dbgM[tt_ * 128:(tt_ + 1) * 128, 0:ncols], NEGM[:, :ncols])
                        for g0 in range(0, nk, 4):
                            tp = trp.next()
                            ng = min(4, nk - g0)
                            for i in range(ng):
                                R.transpose(tp[:, i, :], NEGM[:, (g0 + i) * 128:(g0 + i + 1) * 128], identb.v)
                            R.copy(NEGMT[:, g0:g0 + ng, it * 128:(it + 1) * 128], tp[:, 0:ng, :], eng="scalar")

                def dsa_slot(j):
                    lt = local_tiles(j)
                    for h in range(HDS):
                        KTh, Vh, QTh = ktp.next(), vtp.next(), qtp.next()
                        R.dma(KTh.v, KdT[h])
                        R.dma(Vh.v, Vd[:, h * 128:(h + 1) * 128].rearrange("(kt p) d -> p kt d", p=128))
                        R.dma(QTh.v, QdT[h, :, j * QB:(j + 1) * QB])
                        tiles = [(KTh[:, kt * 128:(kt + 1) * 128], Vh[:, kt, :], NEGMT[:, i, :], None)
                                 for i, kt in enumerate(lt)]
                        attn_unit(pools, QTh, 0, tiles, None,
                                  GT[HF + h, :, j * QB:(j + 1) * QB], YT[HF + h, :, j * QB:(j + 1) * QB])

                GS = 2 if TPB % 2 == 0 else 1
                groups_ = [(j, list(range(i0, i0 + GS))) for j in range(4) for i0 in range(0, TPB, GS)]

                def do_idx(gi):
                    j, its = groups_[gi]
                    if j not in POSKs:
                        POSK = poskp.next()
                        R.dma(POSK[:, 0, :], posk_d[:, j * QB:(j + 1) * QB].partition_broadcast(128))
                        R.dma(POSK[:, 1, :], posk_d[:, (4 + j) * QB:(5 + j) * QB].partition_broadcast(128))
                        POSKs[j] = POSK
                    return [idx_tile(j, it) for it in its]

                pend = do_idx(0)
                for gi in range(len(groups_)):
                    cur = pend
                    if gi + 1 < len(groups_):
                        pend = do_idx(gi + 1)
                    bis_group(cur)
                    j, its = groups_[gi]
                    if its[-1] == TPB - 1:
                        dsa_slot(j)
            R.barrier()

            with ExitStack() as ph:
                pools = attn_pools(ph)
                qtp = mkpool(ph, "QTm", 2, [128, So], BF16)
                for h in range(HM):
                    QTh = qtp.next()
                    R.dma(QTh.v, QmT[h])
                    for j in range(4):
                        tiles = [(KmT[:, h, t * 128:(t + 1) * 128], Vm[:, t, h * 128:(h + 1) * 128], None, None)
                                 for t in range(C.NM // 128)]
                        b = HF + HDS + h
                        attn_unit(pools, QTh, j * QB, tiles, None,
                                  GT[b, :, j * QB:(j + 1) * QB], YT[b, :, j * QB:(j + 1) * QB])
            R.barrier()

            with ExitStack() as ph:
                NB, NCH, CW = C.NBLK, C.NCH, C.CW
                YTs = SB(ph, "YTs", [128, NB, So], BF16)
                Wo = SB(ph, "Wo", [128, NB, D], BF16)
                Grow = SB(ph, "Grow", [128, D], F32)
                Brow = SB(ph, "Brow", [128, D], F32)
                for c in range(NB):
                    R.dma(YTs[:, c, :], YT[c])
                    R.dma(Wo[:, c, :], w_out[c * 128:(c + 1) * 128, :], q="gpsimd")
                R.dma(Grow.v, ln_g.partition_broadcast(128))
                R.dma(Brow.v, ln_b.partition_broadcast(128))
                psF = mkpool(ph, "psF", min(6, 2 * NCH), [128, 512], F32, ps=True)
                trF = mkpool(ph, "trF", 2, [128, 8, 128], BF16, ps=True)
                znbp = mkpool(ph, "znb", 2, [128, D], BF16)
                xtsp = mkpool(ph, "xts", 2, [128, KC, 128], BF16)
                xp = mkpool(ph, "xt", 2, [128, D], F32)
                zpool = mkpool(ph, "zt", 1, [128, D], F32)
                stp = mkpool(ph, "stats", 2, [128, NCH, 6], F32)
                mvp = mkpool(ph, "mv", 2, [128, 2], F32)
                rsp = mkpool(ph, "rstd", 2, [128, 1], F32)
                nmp = mkpool(ph, "nmr", 2, [128, 1], F32)
                outb = Buf(out_d, "out")
                for t in range(NTo):
                    xt = xp.next()
                    R.dma(xt.v, x_own[t * 128:(t + 1) * 128, :])
                    z = zpool.next()
                    stats = stp.next()
                    for n in range(NCH):
                        ps = psF.next()
                        for c in range(NB):
                            R.mm(ps[:, :CW], YTs[:, c, t * 128:(t + 1) * 128], Wo[:, c, n * CW:(n + 1) * CW],
                                 start=(c == 0), stop=(c == NB - 1))
                        zs = z[:, n * CW:(n + 1) * CW]
                        R.stt(zs, xt[:, n * CW:(n + 1) * CW], C.ALPHA, ps[:, :CW], ALU.mult, ALU.add)
                        so, zi = stats[:, n, :], zs
                        R.generic("vector", lambda e, a=so.ap, b=zi.ap: e.bn_stats(a, b), reads=[zs], writes=[stats.v])
                    mv, rstd, nmr = mvp.next(), rsp.next(), nmp.next()
                    R.generic("vector", lambda e, a=mv.ap, b=stats.ap: e.bn_aggr(a, b.rearrange("p n s -> p (n s)")),
                              reads=[stats.v], writes=[mv.v])
                    R.act(rstd.v, mv[:, 1:2], AF.Ln, bias=1e-5)
                    R.act(rstd.v, rstd.v, AF.Exp, scale=-0.5)
                    R.ts(nmr.v, mv[:, 0:1], rstd.v, -1.0, ALU.mult, op1=ALU.mult)
                    zn = xt
                    R.act(zn.v, z.v, AF.Identity, scale=rstd.v, bias=nmr.v)
                    R.tt(zn.v, zn.v, Grow.v, ALU.mult)
                    R.tt(zn.v, zn.v, Brow.v, ALU.add)
                    if last:
                        R.dma(outb[t * 128:(t + 1) * 128, :], zn.v)
                    else:
                        R.dma(xcur[t * 128:(t + 1) * 128, :], zn.v)
                        znb, xts = znbp.next(), xtsp.next()
                        R.copy(znb.v, zn.v, eng="gpsimd")
                        for g0_ in range(0, KC, 8):
                            ng = min(8, KC - g0_)
                            tp = trF.next()
                            for i in range(ng):
                                R.transpose(tp[:, i, :], znb[:, (g0_ + i) * 128:(g0_ + i + 1) * 128], identb.v)
                            R.copy(xts[:, g0_:g0_ + ng, :], tp[:, 0:ng, :], eng="scalar")
                        R.dma(xTo.rearrange("(kc p) s -> p kc s", p=128)[:, :, t * 128:(t + 1) * 128], xts.v)
                if last:
                    R.wait_all("sync", [outb])
            if not last:
                R.barrier()
                for xq in range(NXC):
                    ci = xTo[xq * XCH:(xq + 1) * XCH, :]
                    co = Gx[xq].rearrange("r p s -> (r p) s")
                    R.cc(lambda e, ci=ci, co=co: e.collective_compute(
                        "AllGather", ALU.bypass, replica_groups=groups, ins=[ci.opt()], outs=[co.opt()]))
                R.barrier()
        R.finalize(st)
    return nc, R


def core_layout(C, half):
    own = OWN[half]
    par = OWN[1 - half]
    blocks = own + par
    perm = np.concatenate([np.arange(b * C.QB, (b + 1) * C.QB) for b in blocks])
    return own, par, blocks, perm


def const_inputs(C, half):
    own, par, blocks, perm = core_layout(C, half)
    S, QB, TPB = C.S, C.QB, C.TPB
    pos = perm.astype(np.float32)
    d = np.arange(128)
    f128 = (10000.0 ** (-(np.arange(64, dtype=np.float32)) / np.float32(64))).astype(np.float32)
    f64 = (10000.0 ** (-(np.arange(32, dtype=np.float32)) / np.float32(32))).astype(np.float32)
    ang128 = (pos[None, :] * f128[d % 64][:, None]).astype(np.float32)
    ang64 = (pos[None, :] * f64[(d % 64) % 32][:, None]).astype(np.float32)
    rot128 = np.zeros((128, 128), np.float32)
    for dd in range(64):
        rot128[dd + 64, dd] = -1.0
        rot128[dd, dd + 64] = 1.0
    rot64 = np.zeros((128, 128), np.float32)
    for o in (0, 64):
        for dl in range(32):
            rot64[o + dl + 32, o + dl] = -1.0
            rot64[o + dl, o + dl + 32] = 1.0
    posq = perm[:C.So].reshape(C.NTo, 128).T.astype(np.float32)
    Mb = np.zeros((C.HF, 8, 8), np.float32)
    for b in range(8):
        for b2 in range(8):
            if blocks[b2] < blocks[b]:
                Mb[:, b, b2] = 1.0
    pmask = np.zeros((128, 4 * QB), np.float32)
    for j in range(4):
        if not (par[j] < own[j]):
            pmask[:, j * QB:(j + 1) * QB] = NEGB
    dmask = np.zeros((128, TPB * QB), np.float32)
    s_idx = np.arange(128)[:, None]
    t_idx = np.arange(QB)[None, :]
    for k in range(TPB):
        dmask[:, k * QB:(k + 1) * QB] = np.where(k * 128 + s_idx > t_idx, NEGB, 0.0)
    return {
        "cos128": np.cos(ang128).astype(np.float32), "sin128": np.sin(ang128).astype(np.float32),
        "cos64": np.cos(ang64).astype(np.float32), "sin64": np.sin(ang64).astype(np.float32),
        "rot128": rot128, "rot64": rot64, "ident": np.eye(128, dtype=np.float32),
        "posq": np.ascontiguousarray(posq), "posk": pos[None, :].copy(), "Mb": Mb,
        "pmask": pmask, "dmask": dmask,
    }


def fused_in_maps(C, x, mem, w_in, b_forget, w_kv, w_out, g, b, consts, ncores):
    L = w_in.shape[0]
    shared = {
        "negbf": np.ascontiguousarray(-b_forget.reshape(L, C.HF, 1)),
        "ln_g": np.ascontiguousarray(g.reshape(L, 1, -1)), "ln_b": np.ascontiguousarray(b.reshape(L, 1, -1)),
    }
    for i in range(L):
        shared[f"w_in{i}"] = np.ascontiguousarray(w_in[i])
        shared[f"w_kv{i}"] = np.ascontiguousarray(w_kv[i])
        shared[f"w_out{i}"] = np.ascontiguousarray(w_out[i])
    maps = []
    for core in range(ncores):
        bi, half = core // 2, core % 2
        own, par, blocks, perm = core_layout(C, half)
        xb = x[bi]
        sel = np.zeros((128, 2), np.float32)
        sel[:, 1 - half] = 1.0
        m = {"xT": np.ascontiguousarray(xb[perm].T), "x_own": np.ascontiguousarray(xb[perm[:C.So]]),
             "memT": np.ascontiguousarray(mem[bi].T), "sel": sel}
        m.update(shared)
        m.update(consts[half])
        maps.append(m)
    return maps


_PROG = {}


def run_fused(C, x, mem, w_in, b_forget, w_mem_kv, w_out, ln_gain, ln_bias, runner=None):
    f = lambda a: np.ascontiguousarray(np.asarray(a, dtype=np.float32))
    x, mem = f(x), f(mem)
    B = x.shape[0]
    ncores = 2 * B
    L = w_in.shape[0]
    key = (C.D, C.S, C.QB, L, ncores)
    if key not in _PROG:
        _PROG[key] = build_program(C, L=L, groups=[[2 * i, 2 * i + 1] for i in range(B)])[0]
    nc = _PROG[key]
    consts = {h: const_inputs(C, h) for h in (0, 1)}
    maps = fused_in_maps(C, x, mem, f(w_in), f(b_forget), f(w_mem_kv), f(w_out), f(ln_gain), f(ln_bias), consts, ncores)
    if runner is None:
        res = run_bass_kernel_spmd(nc, maps, core_ids=list(range(ncores))).results
    else:
        res = runner(nc, maps)
    out = np.empty_like(x)
    for core in range(ncores):
        bi, half = core // 2, core % 2
        own, par, blocks, perm = core_layout(C, half)
        out[bi][perm[:C.So]] = np.asarray(res[core]["out"], dtype=np.float32)
    return out


def kernel(x, mem, w_in, b_forget, w_mem_kv, w_out, ln_gain, ln_bias):
    return run_fused(Cfg(), x, mem, w_in, b_forget, w_mem_kv, w_out, ln_gain, ln_bias)
```

```python
from contextlib import ExitStack
from concourse.bass_utils import run_bass_kernel_spmd
import numpy as np
import concourse.bass as bass
import concourse.mybir as mybir

F32 = mybir.dt.float32
BF16 = mybir.dt.bfloat16
ALU = mybir.AluOpType
AF = mybir.ActivationFunctionType

COMPUTE = ("tensor", "vector", "scalar", "gpsimd")


class Buf:
    __slots__ = ("ap", "w", "r", "name")

    def __init__(self, ap, name=""):
        self.ap = ap
        self.w = None
        self.r = []
        self.name = name

    def __getitem__(self, idx):
        return V(self.ap[idx], self)

    @property
    def v(self):
        return V(self.ap, self)


class V:
    __slots__ = ("ap", "buf")

    def __init__(self, ap, buf):
        self.ap = ap
        self.buf = buf

    def __getitem__(self, idx):
        return V(self.ap[idx], self.buf)

    def re(self, pattern, **kw):
        return V(self.ap.rearrange(pattern, **kw), self.buf)

    def bc(self, shape):
        return V(self.ap.to_broadcast(shape), self.buf)


def _ap(x):
    return x.ap if isinstance(x, V) else x


class Op:
    __slots__ = ("eng", "fn", "deps", "idx", "needed", "is_dma", "semval", "dsem", "is_cc")

    def __init__(self, eng, fn, is_dma):
        self.eng = eng
        self.fn = fn
        self.deps = set()
        self.needed = False
        self.is_dma = is_dma
        self.semval = None
        self.dsem = None
        self.is_cc = False


class Rec:
    def __init__(self, nc, n_dma_sems=12):
        self.nc = nc
        self.ops = []
        self.eng_ops = {e: [] for e in ("tensor", "vector", "scalar", "gpsimd", "sync")}
        self.n_dma_sems = n_dma_sems
        self.last_real = {}
        self.dma_since = []

    def emit(self, eng, fn, reads=(), writes=(), is_dma=False):
        op = Op(eng, fn, is_dma)
        oid = len(self.ops)
        for x in reads:
            b = x.buf if isinstance(x, V) else x
            if b is None or not isinstance(b, Buf):
                continue
            if b.w is not None:
                op.deps.add(b.w)
        for x in writes:
            b = x.buf if isinstance(x, V) else x
            if b is None or not isinstance(b, Buf):
                continue
            if b.w is not None:
                op.deps.add(b.w)
            for r in b.r:
                op.deps.add(r)
        op.deps.discard(oid)
        for x in reads:
            b = x.buf if isinstance(x, V) else x
            if isinstance(b, Buf):
                b.r.append(oid)
        for x in writes:
            b = x.buf if isinstance(x, V) else x
            if isinstance(b, Buf):
                b.w = oid
                b.r = []
        self.ops.append(op)
        self.eng_ops[eng].append(oid)
        if is_dma:
            self.dma_since.append(oid)
        elif not getattr(fn, "_waitonly", False):
            self.last_real[eng] = oid
        return oid

    def barrier(self):
        deps = set(self.last_real.values()) | set(self.dma_since)
        self.dma_since = []
        for eng in self.eng_ops:
            fn = lambda e: None
            op = Op(eng, fn, False)
            op.deps = set(deps)
            self.ops.append(op)
            self.eng_ops[eng].append(len(self.ops) - 1)

    def finalize(self, stack):
        nc = self.nc
        ops = self.ops
        esem = {e: stack.enter_context(nc.semaphore(f"s_{e}")) for e in self.eng_ops}
        dsems = {e: [stack.enter_context(nc.semaphore(f"d_{e}_{i}")) for i in range(self.n_dma_sems)]
                 for e in ("sync", "gpsimd", "scalar")}
        dcount = {e: [0] * self.n_dma_sems for e in dsems}
        drot = {e: 0 for e in dsems}
        prev_on_dsem = {}
        ccsem = stack.enter_context(nc.semaphore("s_cc"))
        cccount = 0
        for oid, op in enumerate(ops):
            if op.is_cc:
                cccount += 1
                op.dsem = ccsem
                op.semval = cccount
                op.needed = True
                continue
            if op.is_dma:
                k = drot[op.eng]
                drot[op.eng] = (k + 1) % self.n_dma_sems
                dcount[op.eng][k] += 16
                op.dsem = dsems[op.eng][k]
                op.semval = dcount[op.eng][k]
                key = (op.eng, k)
                if key in prev_on_dsem:
                    op.deps.add(prev_on_dsem[key])
                prev_on_dsem[key] = oid
                op.needed = True
        for oid, op in enumerate(ops):
            nd = set()
            best = {}
            for d in op.deps:
                dop = ops[d]
                if dop.fn is None:
                    continue
                if dop.is_dma:
                    nd.add(d)
                    continue
                if (not op.is_dma) and dop.eng == op.eng == "tensor":
                    continue
                if best.get(dop.eng, -1) < d:
                    best[dop.eng] = d
            nd |= set(best.values())
            op.deps = nd
            for d in nd:
                ops[d].needed = True
        ecount = {e: 0 for e in self.eng_ops}
        for oid, op in enumerate(ops):
            if (not op.is_dma) and op.needed:
                ecount[op.eng] += 1
                op.dsem = esem[op.eng]
                op.semval = ecount[op.eng]
        self.sem_counts = dict(ecount)
        self.n_wait = 0
        rec = self

        def replay(e, ename):
            seen = {}
            for oid in rec.eng_ops[ename]:
                op = ops[oid]
                need = {}
                for d in op.deps:
                    dop = ops[d]
                    s = dop.dsem
                    key = id(s)
                    if seen.get(key, 0) >= dop.semval:
                        continue
                    if key not in need or need[key][1] < dop.semval:
                        need[key] = (s, dop.semval)
                for key, (s, val) in need.items():
                    e.wait_ge(s, val)
                    seen[key] = val
                    rec.n_wait += 1
                ins = op.fn(e)
                if op.needed and ins is not None:
                    if op.is_cc:
                        ins.then_inc(op.dsem)
                    else:
                        ins.then_inc(op.dsem, 16 if op.is_dma else 1)

        block = stack.enter_context(nc.Block())

        @block.sync
        def _(e):
            replay(e, "sync")

        @block.tensor
        def _(e):
            replay(e, "tensor")

        @block.vector
        def _(e):
            replay(e, "vector")

        @block.scalar
        def _(e):
            replay(e, "scalar")

        @block.gpsimd
        def _(e):
            replay(e, "gpsimd")

    def dma(self, out, in_, q="sync", **kw):
        o, i = _ap(out), _ap(in_)
        return self.emit(q, lambda e: e.dma_start(out=o, in_=i, **kw), reads=[in_], writes=[out], is_dma=True)

    def mm(self, out, lhsT, rhs, start=True, stop=True, **kw):
        o, l, r = _ap(out), _ap(lhsT), _ap(rhs)
        return self.emit("tensor", lambda e: e.matmul(o, l, r, start=start, stop=stop, **kw),
                         reads=[lhsT, rhs], writes=[out])

    def transpose(self, out, in_, ident):
        o, i, d = _ap(out), _ap(in_), _ap(ident)
        return self.emit("tensor", lambda e: e.transpose(o, i, d), reads=[in_, ident], writes=[out])

    def act(self, out, in_, func, bias=None, scale=None, accum_out=None, eng="scalar"):
        o, i = _ap(out), _ap(in_)
        kw = {}
        reads = [in_]
        writes = [out]
        if bias is not None:
            kw["bias"] = _ap(bias)
            reads.append(bias)
        if scale is not None:
            kw["scale"] = _ap(scale)
            reads.append(scale)
        if accum_out is not None:
            kw["accum_out"] = _ap(accum_out)
            writes.append(accum_out)
        return self.emit("scalar", lambda e: e.activation(o, i, func, **kw), reads=reads, writes=writes)

    def ts(self, out, in0, s1, s2, op0, op1=None, accum_out=None, eng="vector"):
        o, i = _ap(out), _ap(in0)
        a1, a2 = _ap(s1), _ap(s2)
        reads = [in0, s1, s2]
        writes = [out]
        kw = {}
        if op1 is not None:
            kw["op1"] = op1
        if accum_out is not None:
            kw["accum_out"] = _ap(accum_out)
            writes.append(accum_out)
        return self.emit(eng, lambda e: e.tensor_scalar(o, i, a1, a2, op0, **kw), reads=reads, writes=writes)

    def tt(self, out, in0, in1, op, eng="vector"):
        o, a, b = _ap(out), _ap(in0), _ap(in1)
        return self.emit(eng, lambda e: e.tensor_tensor(o, a, b, op), reads=[in0, in1], writes=[out])

    def stt(self, out, in0, scalar, in1, op0, op1, eng="vector"):
        o, a, s, b = _ap(out), _ap(in0), _ap(scalar), _ap(in1)
        return self.emit(eng, lambda e: e.scalar_tensor_tensor(o, a, s, b, op0, op1),
                         reads=[in0, scalar, in1], writes=[out])

    def copy(self, out, in_, eng="vector"):
        o, i = _ap(out), _ap(in_)
        if eng == "scalar":
            return self.emit(eng, lambda e: e.copy(o, i), reads=[in_], writes=[out])
        return self.emit(eng, lambda e: e.tensor_copy(o, i), reads=[in_], writes=[out])

    def memset(self, out, val, eng="vector"):
        o = _ap(out)
        return self.emit(eng, lambda e: e.memset(o, val), writes=[out])

    def recip(self, out, in_):
        o, i = _ap(out), _ap(in_)
        return self.emit("vector", lambda e: e.reciprocal(o, i), reads=[in_], writes=[out])

    def reduce(self, out, in_, op, axis=mybir.AxisListType.X, eng="vector"):
        o, i = _ap(out), _ap(in_)
        return self.emit(eng, lambda e: e.tensor_reduce(o, i, axis, op), reads=[in_], writes=[out])

    def scan(self, out, d0, d1, initial, op0, op1):
        o, a, b, ini = _ap(out), _ap(d0), _ap(d1), _ap(initial)
        return self.emit("vector", lambda e: e.tensor_tensor_scan(o, a, b, ini, op0, op1),
                         reads=[d0, d1, initial], writes=[out])

    def cc(self, fn):
        oid = self.emit("gpsimd", fn, is_dma=True)
        self.ops[oid].is_cc = True
        return oid

    def wait_all(self, eng, bufs):
        return self.emit(eng, lambda e: None, reads=list(bufs), writes=[])

    def generic(self, eng, fn, reads=(), writes=()):
        return self.emit(eng, fn, reads=reads, writes=writes)

class Cfg:
    def __init__(s, D=2048, S=4096, QB=512, HF=6, HDS=6, HM=4, IH=16, ID=64, TOPK=256, NM=256, NIT=20):
        s.D, s.S, s.QB, s.HF, s.HDS, s.HM, s.IH, s.ID, s.TOPK, s.NM, s.NIT = D, S, QB, HF, HDS, HM, IH, ID, TOPK, NM, NIT
        s.HD = 128
        s.DF, s.DD, s.DM = HF * 128, HDS * 128, HM * 128
        s.DMIX = s.DF + s.DD + s.DM
        sp = (s.DF, s.DF, s.DF, s.DF, HF, s.DD, s.DD, s.DD, s.DD, IH * ID, ID, IH, s.DM, s.DM)
        names = ("fq", "fk", "fv", "fg", "fl", "dq", "dk", "dv", "dg", "iq", "ik", "iw", "mq", "mg")
        off = 0
        s.col = {}
        for n, w in zip(names, sp):
            s.col[n] = (off, w)
            off += w
        s.NIN = off
        s.KC = D // 128
        s.So = S // 2
        assert S == 8 * QB and QB % 128 == 0 and QB <= 512
        s.TPB = QB // 128
        s.NT = S // 128
        s.NTo = s.So // 128
        s.NBLK = s.DMIX // 128
        s.SCALE = 128 ** -0.5
        s.WSCALE = (IH ** -0.5) * (ID ** -0.5)
        s.ALPHA = 8 ** 0.25
        s.NCH = max(1, D // 512)
        s.CW = min(512, D)
        assert IH % 2 == 0 and ID == 64


OWN = {0: [0, 3, 4, 7], 1: [1, 2, 5, 6]}
NEGB = -30000.0


def build_program(C, L=4, groups=None, LW=None):
    LW = LW or L
    nc = bass.Bass("TRN2", target_bir_lowering=False)
    D, S, QB, So, KC, NT, NTo, TPB = C.D, C.S, C.QB, C.So, C.KC, C.NT, C.NTo, C.TPB
    HF, HDS, HM, IH = C.HF, C.HDS, C.HM, C.IH

    def din(name, shape, dt=F32):
        return nc.dram_tensor(name, list(shape), dt, kind="ExternalInput").ap()

    def dscr(name, shape, dt):
        return nc.dram_tensor(name, list(shape), dt, kind="Internal").ap()

    xT = din("xT", [D, S])
    x_own0 = din("x_own", [So, D])
    memT = din("memT", [D, C.NM])
    w_in_all = [din(f"w_in{i}", [D, C.NIN]) for i in range(LW)]
    w_kv_all = [din(f"w_kv{i}", [D, 2 * C.DM]) for i in range(LW)]
    w_out_all = [din(f"w_out{i}", [C.DMIX, D]) for i in range(LW)]
    negbf_all = din("negbf", [LW, HF, 1])
    ln_g_all = din("ln_g", [LW, 1, D])
    ln_b_all = din("ln_b", [LW, 1, D])
    sel_d = din("sel", [128, 2])
    if groups is None:
        groups = [[0, 1], [2, 3], [4, 5], [6, 7]]
    cos128 = din("cos128", [128, S])
    sin128 = din("sin128", [128, S])
    cos64 = din("cos64", [128, S])
    sin64 = din("sin64", [128, S])
    rot128 = din("rot128", [128, 128])
    rot64 = din("rot64", [128, 128])
    ident_d = din("ident", [128, 128])
    posq_d = din("posq", [128, NTo])
    posk_d = din("posk", [1, S])
    Mb_d = din("Mb", [HF, 8, 8])
    pmask_d = din("pmask", [128, 4 * QB])
    dmask_d = din("dmask", [128, TPB * QB])
    out_d = nc.dram_tensor("out", [So, D], F32, kind="ExternalOutput").ap()

    QfT = dscr("QfT", [HF, 128, So], BF16)
    KfT = dscr("KfT", [HF, 128, S], BF16)
    QdT = dscr("QdT", [HDS, 128, So], BF16)
    KdT = dscr("KdT", [HDS, 128, S], BF16)
    QiT = dscr("QiT", [IH // 2, 128, So], BF16)
    KiTd = dscr("KiT", [128, S], BF16)
    QmT = dscr("QmT", [HM, 128, So], BF16)
    GT = dscr("GT", [C.NBLK, 128, So], F32)
    Vf = dscr("Vf", [S, C.DF], BF16)
    Vd = dscr("Vd", [S, C.DD], BF16)
    CnD = dscr("CnD", [HF, S], F32)
    YT = dscr("YT", [C.NBLK, 128, So], BF16)
    LFD = dscr("LFD", [HF, S], F32)
    CBD = dscr("CBD", [3, HF, So], BF16)
    dbgM = dscr("dbgM", [So, S], BF16) if getattr(C, "DBG", False) else None
    xcur = dscr("xcur", [So, D], F32)
    xTo = dscr("xTo", [D, So], BF16)
    XCH = min(getattr(C, "XCH", 512), D)
    NXC = D // XCH
    Gx = dscr("Gx", [NXC, 2, XCH, So], BF16)

    with ExitStack() as st:
        R = Rec(nc)

        uid = [0]

        def SB(stk, name, shape, dt):
            uid[0] += 1
            t = stk.enter_context(nc.sbuf_tensor(f"{name}_{uid[0]}", list(shape), dt))
            return Buf(t.ap(), name)

        def PS(stk, name, shape, dt):
            uid[0] += 1
            t = stk.enter_context(nc.psum_tensor(f"{name}_{uid[0]}", list(shape), dt))
            return Buf(t.ap(), name)

        class Pool:
            def __init__(s, bufs):
                s.bufs, s.i = bufs, 0

            def next(s):
                b = s.bufs[s.i % len(s.bufs)]
                s.i += 1
                return b

        def mkpool(stk, name, n, shape, dt, ps=False):
            return Pool([(PS if ps else SB)(stk, f"{name}{i}", shape, dt) for i in range(n)])

        identb = SB(st, "identb", [128, 128], BF16)
        rot128b = SB(st, "rot128b", [128, 128], BF16)
        rot64b = SB(st, "rot64b", [128, 128], BF16)
        ones128 = SB(st, "ones128", [128, 128], BF16)
        pmaskb = SB(st, "pmaskb", [128, 4 * QB], BF16)
        dmaskb = SB(st, "dmaskb", [128, TPB * QB], BF16)
        POSQ = SB(st, "POSQ", [128, NTo], F32)
        WI = SB(st, "WI", [128, NTo, IH], F32)
        NEGBF = SB(st, "NEGBF", [HF, 1], F32)
        KmT = SB(st, "KmT", [128, HM, C.NM], BF16)
        Vm = SB(st, "Vm", [128, C.NM // 128, C.DM], BF16)

        R.dma(identb.v, ident_d, q="gpsimd")
        R.dma(rot128b.v, rot128, q="gpsimd")
        R.dma(rot64b.v, rot64, q="gpsimd")
        R.dma(pmaskb.v, pmask_d, q="gpsimd")
        R.dma(dmaskb.v, dmask_d, q="gpsimd")
        R.dma(POSQ.v, posq_d)
        SEL = SB(st, "SEL", [128, 2], F32)
        R.dma(SEL.v, sel_d)
        R.memset(ones128.v, 1.0)

        for l in range(L):
            w_in, w_kv, w_out = w_in_all[l % LW], w_kv_all[l % LW], w_out_all[l % LW]
            ln_g, ln_b = ln_g_all[l % LW], ln_b_all[l % LW]
            x_own = x_own0 if l == 0 else xcur
            last = (l == L - 1)
            R.dma(NEGBF.v, negbf_all[l % LW])
            with ExitStack() as ph:
                XT = SB(ph, "XT", [128, KC, S], BF16)
                wpool = mkpool(ph, "Wt", 2, [128, KC, 512], BF16)
                psA = mkpool(ph, "psA", 4, [128, 512], F32, ps=True)
                psR = mkpool(ph, "psR", 2, [128, 512], F32, ps=True)
                stb = mkpool(ph, "stb", 3, [128, 512], BF16)
                stf = mkpool(ph, "stf", 2, [128, 512], F32)
                tmpb = mkpool(ph, "tmpb", 2, [128, 512], BF16)
                tmpa = mkpool(ph, "tmpa", 1, [128, 512], F32)
                tmpc = mkpool(ph, "tmpc", 1, [128, 512], F32)
                cst = mkpool(ph, "cst", 2, [128, 512], F32)
                snt = mkpool(ph, "snt", 2, [128, 512], F32)
                evac_i = [0]
                if l == 0:
                    for kc in range(KC):
                        R.dma(XT[:, kc, :], xT[kc * 128:(kc + 1) * 128, :], q="gpsimd")
                else:
                    for kc in range(KC):
                        R.dma(XT[:, kc, 0:So], xTo[kc * 128:(kc + 1) * 128, :])
                        for tb in range(So // QB):
                            g0, g1, tm_ = stb.next(), stb.next(), tmpb.next()
                            xq, xr = (kc * 128) // XCH, (kc * 128) % XCH
                            R.dma(g0[:, :QB], Gx[xq, 0, xr:xr + 128, tb * QB:(tb + 1) * QB])
                            R.dma(g1[:, :QB], Gx[xq, 1, xr:xr + 128, tb * QB:(tb + 1) * QB])
                            R.ts(tm_[:, :QB], g0[:, :QB], SEL[:, 0:1], None, ALU.mult)
                            R.stt(XT[:, kc, So + tb * QB:So + (tb + 1) * QB], g1[:, :QB], SEL[:, 1:2], tm_[:, :QB],
                                  ALU.mult, ALU.add)

                def load_w(src, c0, w, dup=False):
                    Wt = wpool.next()
                    srcv = src[:, c0:c0 + w].rearrange("(kc p) n -> p kc n", p=128)
                    R.dma(Wt[:, :, 0:w], srcv, q="gpsimd")
                    if dup:
                        R.dma(Wt[:, :, w:2 * w], srcv, q="gpsimd")
                    return Wt

                def tform(Wt, coff, M, src_xt, tb, width):
                    ps = psA.next()
                    for kc in range(KC):
                        R.mm(ps[:M, :width], Wt[:, kc, coff:coff + M], src_xt[:, kc, tb * width:(tb + 1) * width],
                             start=(kc == 0), stop=(kc == KC - 1))
                    return ps

                def evac_bf(dst, src, scale=None):
                    evac_i[0] += 1
                    if scale is None and evac_i[0] % 2 == 0:
                        R.copy(dst, src, eng="vector")
                    else:
                        R.act(dst, src, AF.Identity, scale=(1.0 if scale is None else scale))

                def rope_post(ps, tb, scale, rotb, cosd, sind, dst):
                    t0 = tmpb.next()
                    R.act(t0[:, :QB], ps[:, :QB], AF.Identity, scale=scale)
                    pr = psR.next()
                    R.mm(pr[:, :QB], rotb.v, t0[:, :QB])
                    c = cst.next()
                    sn = snt.next()
                    R.dma(c[:, :QB], cosd[:, tb * QB:(tb + 1) * QB])
                    R.dma(sn[:, :QB], sind[:, tb * QB:(tb + 1) * QB])
                    a = tmpa.next()
                    b = tmpc.next()
                    R.tt(a[:, :QB], t0[:, :QB], c[:, :QB], ALU.mult)
                    R.tt(b[:, :QB], pr[:, :QB], sn[:, :QB], ALU.mult)
                    o = stb.next()
                    R.tt(o[:, :QB], a[:, :QB], b[:, :QB], ALU.add)
                    R.dma(dst, o[:, :QB])

                def seg_T(name, kind, dst, ntb, nblocks=None):
                    c0, w = C.col[name]
                    nb = w // 128
                    for s0 in range(0, nb, 4):
                        nbl = min(4, nb - s0)
                        Wt = load_w(w_in, c0 + s0 * 128, nbl * 128)
                        for bl in range(nbl):
                            blk = s0 + bl
                            for tb in range(ntb):
                                ps = tform(Wt, bl * 128, 128, XT, tb, QB)
                                d = dst(blk, tb)
                                if kind == "q":
                                    o = stb.next()
                                    R.act(o[:, :QB], ps[:, :QB], AF.Identity, scale=C.SCALE)
                                    R.dma(d, o[:, :QB])
                                elif kind == "k":
                                    o = stb.next()
                                    evac_bf(o[:, :QB], ps[:, :QB])
                                    R.dma(d, o[:, :QB])
                                elif kind == "gate":
                                    o = stf.next()
                                    R.act(o[:, :QB], ps[:, :QB], AF.Silu)
                                    R.dma(d, o[:, :QB])
                                elif kind == "qr":
                                    rope_post(ps, tb, C.SCALE, rot128b, cos128, sin128, d)
                                elif kind == "kr":
                                    rope_post(ps, tb, 1.0, rot128b, cos128, sin128, d)
                                elif kind == "iq":
                                    rope_post(ps, tb, 1.0, rot64b, cos64, sin64, d)

                def seg_N(name, dst_d):
                    c0, w = C.col[name]
                    for s0 in range(0, w, 512):
                        ww = min(512, w - s0)
                        Wt = load_w(w_in, c0 + s0, ww)
                        for t in range(NT):
                            ps = psA.next()
                            for kc in range(KC):
                                R.mm(ps[:, :ww], XT[:, kc, t * 128:(t + 1) * 128], Wt[:, kc, 0:ww],
                                     start=(kc == 0), stop=(kc == KC - 1))
                            o = stb.next()
                            evac_bf(o[:, :ww], ps[:, :ww])
                            R.dma(dst_d[t * 128:(t + 1) * 128, s0:s0 + ww], o[:, :ww])

                NO = So // QB
                NA = S // QB
                sl = lambda tb: slice(tb * QB, (tb + 1) * QB)
                seg_T("fq", "q", lambda b, tb: QfT[b, :, sl(tb)], NO)
                seg_T("fk", "k", lambda b, tb: KfT[b, :, sl(tb)], NA)
                seg_N("fv", Vf)
                seg_T("fg", "gate", lambda b, tb: GT[b, :, sl(tb)], NO)
                c0, w = C.col["fl"]
                Wt = load_w(w_in, c0, HF)
                for tb in range(NA):
                    ps = tform(Wt, 0, HF, XT, tb, QB)
                    e = stf.next()
                    R.act(e[:HF, :QB], ps[:HF, :QB], AF.Exp, scale=-1.0, bias=NEGBF.v)
                    e2 = stf.next()
                    R.act(e2[:HF, :QB], e[:HF, :QB], AF.Ln, bias=1.0)
                    R.dma(LFD[:, sl(tb)], e2[:HF, :QB])
                seg_T("dq", "qr", lambda b, tb: QdT[b, :, sl(tb)], NO)
                seg_T("dk", "kr", lambda b, tb: KdT[b, :, sl(tb)], NA)
                seg_N("dv", Vd)
                seg_T("dg", "gate", lambda b, tb: GT[HF + b, :, sl(tb)], NO)
                seg_T("iq", "iq", lambda b, tb: QiT[b, :, sl(tb)], NO)
                c0, w = C.col["ik"]
                Wt = load_w(w_in, c0, 64, dup=True)
                for tb in range(NA):
                    ps = tform(Wt, 0, 128, XT, tb, QB)
                    rope_post(ps, tb, 1.0, rot64b, cos64, sin64, KiTd[:, sl(tb)])
                c0, w = C.col["iw"]
                Wt = load_w(w_in, c0, IH)
                for t in range(NTo):
                    ps = psA.next()
                    for kc in range(KC):
                        R.mm(ps[:, :IH], XT[:, kc, t * 128:(t + 1) * 128], Wt[:, kc, 0:IH],
                             start=(kc == 0), stop=(kc == KC - 1))
                    R.act(WI[:, t, :], ps[:, :IH], AF.Identity, scale=C.WSCALE)
                seg_T("mq", "q", lambda b, tb: QmT[b, :, sl(tb)], NO)
                seg_T("mg", "gate", lambda b, tb: GT[HF + HDS + b, :, sl(tb)], NO)
            R.barrier()
            with ExitStack() as ph:
                MT = SB(ph, "MT", [128, KC, C.NM], BF16)
                for kc in range(KC):
                    R.dma(MT[:, kc, :], memT[kc * 128:(kc + 1) * 128, :], q="gpsimd")
                wpool = mkpool(ph, "Wm", 2, [128, KC, 512], BF16)
                psA = mkpool(ph, "psM", 4, [128, 512], F32, ps=True)
                evac_i = [0]

                def load_w(src, c0, w):
                    Wt = wpool.next()
                    R.dma(Wt[:, :, 0:w], src[:, c0:c0 + w].rearrange("(kc p) n -> p kc n", p=128), q="gpsimd")
                    return Wt

                def tform(Wt, coff, M, src_xt, tb, width):
                    ps = psA.next()
                    for kc in range(KC):
                        R.mm(ps[:M, :width], Wt[:, kc, coff:coff + M], src_xt[:, kc, tb * width:(tb + 1) * width],
                             start=(kc == 0), stop=(kc == KC - 1))
                    return ps

                def evac_bf(dst, src):
                    evac_i[0] += 1
                    if evac_i[0] % 2 == 0:
                        R.copy(dst, src, eng="vector")
                    else:
                        R.act(dst, src, AF.Identity, scale=1.0)
                for s0 in range(0, HM, 4):
                    nbl = min(4, HM - s0)
                    Wt = load_w(w_kv, s0 * 128, nbl * 128)
                    for bl in range(nbl):
                        ps = tform(Wt, bl * 128, 128, MT, 0, C.NM)
                        evac_bf(KmT[:, s0 + bl, :], ps[:, :C.NM])
                for s0 in range(0, C.DM, 512):
                    ww = min(512, C.DM - s0)
                    Wt = load_w(w_kv, C.DM + s0, ww)
                    for t in range(C.NM // 128):
                        ps = psA.next()
                        for kc in range(KC):
                            R.mm(ps[:, :ww], MT[:, kc, t * 128:(t + 1) * 128], Wt[:, kc, 0:ww],
                                 start=(kc == 0), stop=(kc == KC - 1))
                        evac_bf(Vm[:, t, s0:s0 + ww], ps[:, :ww])
            R.barrier()

            with ExitStack() as ph:
                onesf = SB(ph, "onesf", [HF, QB], F32)
                Cw = SB(ph, "Cw", [HF, S], F32)
                Cn = SB(ph, "Cn", [HF, S], F32)
                Mb = SB(ph, "Mb", [HF, 8, 8], F32)
                tmp88 = SB(ph, "tmp88", [HF, 8, 8], F32)
                tot = SB(ph, "tot", [HF, 8], F32)
                off = SB(ph, "off", [HF, 8], F32)
                LF = SB(ph, "LF", [HF, S], F32)
                R.dma(LF.v, LFD)
                R.memset(onesf.v, 1.0)
                R.dma(Mb.v, Mb_d)
                for b in range(8):
                    R.scan(Cw[:, b * QB:(b + 1) * QB], onesf.v, LF[:, b * QB:(b + 1) * QB], 0.0, ALU.mult, ALU.add)
                Cw3 = Cw.v.re("h (b q) -> h b q", q=QB)
                R.copy(tot.v, Cw3[:, :, QB - 1])
                totb = V(tot.ap.unsqueeze(1).to_broadcast([HF, 8, 8]), tot)
                R.tt(tmp88.v, Mb.v, totb, ALU.mult)
                R.reduce(off.v, tmp88.v, ALU.add)
                offb = V(off.ap.unsqueeze(2).to_broadcast([HF, 8, QB]), off)
                R.tt(Cn.v.re("h (b q) -> h b q", q=QB), Cw3, offb, ALU.add)
                CnDb = Buf(CnD, "CnD")
                R.dma(CnDb.v, Cn.v)
                rowp = mkpool(ph, "row", 2, [1, So], F32)
                r1p = mkpool(ph, "r1", 2, [1, So], F32)
                hip = mkpool(ph, "hi", 2, [1, So], BF16)
                midp = mkpool(ph, "mid", 2, [1, So], BF16)
                lop = mkpool(ph, "lo", 2, [1, So], BF16)
                for h in range(HF):
                    row, r1, hi, mid, lo = rowp.next(), r1p.next(), hip.next(), midp.next(), lop.next()
                    R.dma(row.v, CnDb[h:h + 1, 0:So])
                    R.ts(hi.v, row.v, -1.0, None, ALU.mult)
                    R.stt(r1.v, row.v, -1.0, hi.v, ALU.mult, ALU.subtract)
                    R.copy(mid.v, r1.v)
                    R.tt(r1.v, r1.v, mid.v, ALU.subtract)
                    R.copy(lo.v, r1.v)
                    R.dma(CBD[0, h:h + 1, :], hi.v)
                    R.dma(CBD[1, h:h + 1, :], mid.v)
                    R.dma(CBD[2, h:h + 1, :], lo.v)
            R.barrier()

            def attn_unit(pools, QTh, qcol0, tiles, cb, gate_src, out_dst):
                Sp, Pp, Op, Up, misc = pools
                Ops, Ups = Op.next(), Up.next()
                n = len(tiles)

                def qk(i):
                    kt, vt, mk, bs = tiles[i]
                    sp = Sp.next()
                    last_qk = (cb is None and mk is None)
                    R.mm(sp[:, :QB], kt, QTh[:, qcol0:qcol0 + QB], start=True, stop=last_qk)
                    if cb is not None:
                        R.mm(sp[:, :QB], ones128[0:3, :], cb, start=False, stop=(mk is None))
                    if mk is not None:
                        R.mm(sp[:, :QB], identb.v, mk, start=False, stop=True)
                    return sp

                sps = {0: qk(0)}
                for i in range(n):
                    if i + 1 < n:
                        sps[i + 1] = qk(i + 1)
                    kt, vt, mk, bs = tiles[i]
                    p = Pp.next()
                    sp = sps.pop(i)
                    if bs is not None:
                        R.act(p[:, :QB], sp[:, :QB], AF.Exp, bias=bs)
                    else:
                        R.act(p[:, :QB], sp[:, :QB], AF.Exp)
                    R.mm(Ops[:, :QB], vt, p[:, :QB], start=(i == 0), stop=(i == n - 1))
                    R.mm(Ups[:, :QB], ones128.v, p[:, :QB], start=(i == 0), stop=(i == n - 1))
                rs, y1, g, yb = [m.next() for m in misc]
                R.dma(g[:, :QB], gate_src)
                R.recip(rs[:, :QB], Ups[:, :QB])
                R.tt(y1[:, :QB], Ops[:, :QB], rs[:, :QB], ALU.mult)
                R.tt(yb[:, :QB], y1[:, :QB], g[:, :QB], ALU.mult)
                R.dma(out_dst, yb[:, :QB])

            def attn_pools(ph, nO=2, shared=None):
                if shared is not None:
                    Sp, Op, Up = Pool(shared[0:2]), Pool(shared[2:3]), Pool(shared[3:4])
                else:
                    Sp = mkpool(ph, "Sps", 2, [128, 512], F32, ps=True)
                    Op = mkpool(ph, "Ops", nO, [128, 512], F32, ps=True)
                    Up = mkpool(ph, "Ups", nO, [128, 512], F32, ps=True)
                Pp = mkpool(ph, "Pt", 3, [128, 512], BF16)
                misc = [mkpool(ph, "rs", 2, [128, 512], F32), mkpool(ph, "y1", 2, [128, 512], F32),
                        mkpool(ph, "gt", 2, [128, 512], F32), mkpool(ph, "yb", 2, [128, 512], BF16)]
                return (Sp, Pp, Op, Up, misc)

            def local_tiles(j):
                own = list(range(0, (j + 1) * TPB))
                par = list(range(4 * TPB, (5 + j) * TPB))
                return own + par

            with ExitStack() as ph:
                pools = attn_pools(ph)
                ktp = mkpool(ph, "KTh", 2, [128, S], BF16)
                vtp = mkpool(ph, "Vh", 2, [128, NT, 128], BF16)
                qtp = mkpool(ph, "QTh", 2, [128, So], BF16)
                CS = SB(ph, "CS", [128, NT, HF], F32)
                CB = SB(ph, "CB", [3, HF, So], BF16)
                for kt in range(NT):
                    R.dma(CS[:, kt, :], CnD[:, kt * 128:(kt + 1) * 128].rearrange("h p -> p h"),
                          allow_slow_non_contiguous=True)
                R.dma(CB.v, CBD)
                for h in range(HF):
                    KTh, Vh, QTh = ktp.next(), vtp.next(), qtp.next()
                    R.dma(KTh.v, KfT[h])
                    R.dma(Vh.v, Vf[:, h * 128:(h + 1) * 128].rearrange("(kt p) d -> p kt d", p=128))
                    R.dma(QTh.v, QfT[h])
                    for j in range(4):
                        tiles = []
                        lt = local_tiles(j)
                        for i, kt in enumerate(lt):
                            mk = None
                            if j * TPB <= i < (j + 1) * TPB:
                                mk = dmaskb[:, (i - j * TPB) * QB:(i - j * TPB + 1) * QB]
                            elif i >= (2 * j + 1) * TPB:
                                mk = pmaskb[:, j * QB:(j + 1) * QB]
                            tiles.append((KTh[:, kt * 128:(kt + 1) * 128], Vh[:, kt, :], mk, CS[:, kt, h:h + 1]))
                        attn_unit(pools, QTh, j * QB, tiles, CB[0:3, h, j * QB:(j + 1) * QB],
                                  GT[h, :, j * QB:(j + 1) * QB], YT[h, :, j * QB:(j + 1) * QB])
            R.barrier()

            with ExitStack() as ph:
                shb = [PS(ph, f"shb{i}", [128, 512], F32) for i in range(5)]
                pools = attn_pools(ph, nO=1, shared=shb)
                zp = Pool(shb)
                accp = mkpool(ph, "accps", 2, [128, 512], F32, ps=True)
                trp = mkpool(ph, "trp", 1, [128, 4, 128], BF16, ps=True)
                ktp = mkpool(ph, "KTd", 1, [128, S], BF16)
                vtp = mkpool(ph, "Vdh", 1, [128, NT, 128], BF16)
                qtp = mkpool(ph, "QTd", 2, [128, QB], BF16)
                KiT = SB(ph, "KiTs", [128, S], BF16)
                poskp = mkpool(ph, "POSK", 2, [128, 2, QB], F32)
                Ip = mkpool(ph, "Ib", 4, [128, S], F32)
                NEGMp = mkpool(ph, "NEGM", 2, [128, S], BF16)
                NEGMT = SB(ph, "NEGMT", [128, NT, QB], BF16)
                rp = mkpool(ph, "relu", 6, [128, 512], BF16)
                tmpm = mkpool(ph, "tmpm", 2, [128, 512], F32)
                qip = mkpool(ph, "QiTt", 2, [128, IH // 2, 128], BF16)
                dgp = mkpool(ph, "DG", 2, [128, IH, 128], BF16)
                sc = {n: mkpool(ph, n, 4, [128, 1], F32) for n in ("mx", "mn", "lo", "w0", "mid", "cnt")}
                predp = mkpool(ph, "pred", 4, [128, 1], mybir.dt.uint32)
                R.dma(KiT.v, KiTd)
                POSKs = {}

                def idx_tile(j, it):
                    tt_ = j * TPB + it
                    nch = 2 * (j + 1)
                    Ib, Qi, DG = Ip.next(), qip.next(), dgp.next()
                    R.dma(Qi.v, QiT[:, :, tt_ * 128:(tt_ + 1) * 128].rearrange("g p s -> p g s"))
                    idb = V(identb.ap.unsqueeze(1).to_broadcast([128, IH, 128]), identb)
                    wib = V(WI.ap[:, tt_, :].unsqueeze(2).to_broadcast([128, IH, 128]), WI)
                    R.tt(DG.v, idb, wib, ALU.mult, eng="gpsimd")
                    for c in range(nch):
                        kcol0 = c * QB if c < j + 1 else 4 * QB + (c - (j + 1)) * QB
                        acc = accp.next()

                        def zmm(h):
                            z = zp.next()
                            po = (h % 2) * 64
                            R.mm(z[:, :QB], Qi[po:po + 64, h // 2, :], KiT[po:po + 64, kcol0:kcol0 + QB])
                            return z

                        zs = {}
                        nxt = 0
                        for h in range(IH):
                            while nxt <= min(h + 4, IH - 1):
                                zs[nxt] = zmm(nxt)
                                nxt += 1
                            r = rp.next()
                            R.act(r[:, :QB], zs.pop(h)[:, :QB], AF.Relu)
                            R.mm(acc[:, :QB], DG[:, h, :], r[:, :QB], start=(h == 0), stop=(h == IH - 1))
                        R.copy(Ib[:, c * QB:(c + 1) * QB], acc[:, :QB], eng="vector")
                    return dict(j=j, it=it, tt=tt_, Ib=Ib, ncols=nch * QB, nk=nch * TPB)

                def bis_group(tiles):
                    for T_ in tiles:
                        j, tt_, Ib, ncols = T_["j"], T_["tt"], T_["Ib"], T_["ncols"]
                        for n_ in ("mx", "mn", "lo", "w0", "mid", "cnt"):
                            T_[n_] = sc[n_].next()
                        T_["NEGM"] = NEGMp.next()
                        R.reduce(T_["mx"].v, Ib[:, :ncols], ALU.max)
                        R.reduce(T_["mn"].v, Ib[:, :ncols], ALU.min)
                        POSK = POSKs[j]
                        for c, pi in ((j, 0), (2 * j + 1, 1)):
                            tm = tmpm.next()
                            R.ts(tm[:, :QB], POSK[:, pi, :], POSQ[:, tt_:tt_ + 1], -2e30, ALU.is_gt, op1=ALU.mult)
                            R.tt(Ib[:, c * QB:(c + 1) * QB], Ib[:, c * QB:(c + 1) * QB], tm[:, :QB], ALU.add)
                        R.ts(T_["lo"].v, T_["mn"].v, -1.0, None, ALU.add)
                        R.ts(T_["w0"].v, T_["mx"].v, 1.0, T_["lo"].v, ALU.add, op1=ALU.subtract)
                    for k in range(C.NIT):
                        for T_ in tiles:
                            R.ts(T_["mid"].v, T_["w0"].v, 0.5 ** (k + 1), T_["lo"].v, ALU.mult, op1=ALU.add)
                        for T_ in tiles:
                            R.ts(T_["NEGM"][:, :T_["ncols"]], T_["Ib"][:, :T_["ncols"]], T_["mid"].v, None, ALU.is_ge,
                                 op1=ALU.add, accum_out=T_["cnt"].v)
                        for T_ in tiles:
                            pred = predp.next()
                            T_["pred"] = pred
                            R.ts(pred.v, T_["cnt"].v, float(min(C.TOPK, S // 4)), None, ALU.is_ge)
                        for T_ in tiles:
                            lo_ap, pr_ap, mid_ap = T_["lo"].ap, T_["pred"].ap, T_["mid"].ap
                            R.generic("vector", lambda e, a=lo_ap, b=pr_ap, c_=mid_ap: e.copy_predicated(a, b, c_),
                                      reads=[T_["pred"].v, T_["mid"].v, T_["lo"].v], writes=[T_["lo"].v])
                    for T_ in tiles:
                        ncols, nk, it, tt_ = T_["ncols"], T_["nk"], T_["it"], T_["tt"]
                        NEGM = T_["NEGM"]
                        R.ts(NEGM[:, :ncols], T_["Ib"][:, :ncols], T_["lo"].v, NEGB, ALU.is_lt, op1=ALU.mult)
                        if dbgM is not None:
                            R.dma(dbgM[tt_ * 128:(tt_ + 1) * 128, 0:ncols], NEGM[:, :ncols])
                        for g0 in range(0, nk, 4):
                            tp = trp.next()
                            ng = min(4, nk - g0)
                            for i in range(ng):
                                R.transpose(tp[:, i, :], NEGM[:, (g0 + i) * 128:(g0 + i + 1) * 128], identb.v)
                            R.copy(NEGMT[:, g0:g0 + ng, it * 128:(it + 1) * 128], tp[:, 0:ng, :], eng="scalar")

                def dsa_slot(j):
                    lt = local_tiles(j)
                    for h in range(HDS):
                        KTh, Vh, QTh = ktp.next(), vtp.next(), qtp.next()
                        R.dma(KTh.v, KdT[h])
                        R.dma(Vh.v, Vd[:, h * 128:(h + 1) * 128].rearrange("(kt p) d -> p kt d", p=128))
                        R.dma(QTh.v, QdT[h, :, j * QB:(j + 1) * QB])
                        tiles = [(KTh[:, kt * 128:(kt + 1) * 128], Vh[:, kt, :], NEGMT[:, i, :], None)
                                 for i, kt in enumerate(lt)]
                        attn_unit(pools, QTh, 0, tiles, None,
                                  GT[HF + h, :, j * QB:(j + 1) * QB], YT[HF + h, :, j * QB:(j + 1) * QB])

                GS = 2 if TPB % 2 == 0 else 1
                groups_ = [(j, list(range(i0, i0 + GS))) for j in range(4) for i0 in range(0, TPB, GS)]

                def do_idx(gi):
                    j, its = groups_[gi]
                    if j not in POSKs:
                        POSK = poskp.next()
                        R.dma(POSK[:, 0, :], posk_d[:, j * QB:(j + 1) * QB].partition_broadcast(128))
                        R.dma(POSK[:, 1, :], posk_d[:, (4 + j) * QB:(5 + j) * QB].partition_broadcast(128))
                        POSKs[j] = POSK
                    return [idx_tile(j, it) for it in its]

                pend = do_idx(0)
                for gi in range(len(groups_)):
                    cur = pend
                    if gi + 1 < len(groups_):
                        pend = do_idx(gi + 1)
                    bis_group(cur)
                    j, its = groups_[gi]
                    if its[-1] == TPB - 1:
                        dsa_slot(j)
            R.barrier()

            with ExitStack() as ph:
                pools = attn_pools(ph)
                qtp = mkpool(ph, "QTm", 2, [128, So], BF16)
                for h in range(HM):
                    QTh = qtp.next()
                    R.dma(QTh.v, QmT[h])
                    for j in range(4):
                        tiles = [(KmT[:, h, t * 128:(t + 1) * 128], Vm[:, t, h * 128:(h + 1) * 128], None, None)
                                 for t in range(C.NM // 128)]
                        b = HF + HDS + h
                        attn_unit(pools, QTh, j * QB, tiles, None,
                                  GT[b, :, j * QB:(j + 1) * QB], YT[b, :, j * QB:(j + 1) * QB])
            R.barrier()

            with ExitStack() as ph:
                NB, NCH, CW = C.NBLK, C.NCH, C.CW
                YTs = SB(ph, "YTs", [128, NB, So], BF16)
                Wo = SB(ph, "Wo", [128, NB, D], BF16)
                Grow = SB(ph, "Grow", [128, D], F32)
                Brow = SB(ph, "Brow", [128, D], F32)
                for c in range(NB):
                    R.dma(YTs[:, c, :], YT[c])
                    R.dma(Wo[:, c, :], w_out[c * 128:(c + 1) * 128, :], q="gpsimd")
                R.dma(Grow.v, ln_g.partition_broadcast(128))
                R.dma(Brow.v, ln_b.partition_broadcast(128))
                psF = mkpool(ph, "psF", min(6, 2 * NCH), [128, 512], F32, ps=True)
                trF = mkpool(ph, "trF", 2, [128, 8, 128], BF16, ps=True)
                znbp = mkpool(ph, "znb", 2, [128, D], BF16)
                xtsp = mkpool(ph, "xts", 2, [128, KC, 128], BF16)
                xp = mkpool(ph, "xt", 2, [128, D], F32)
                zpool = mkpool(ph, "zt", 1, [128, D], F32)
                stp = mkpool(ph, "stats", 2, [128, NCH, 6], F32)
                mvp = mkpool(ph, "mv", 2, [128, 2], F32)
                rsp = mkpool(ph, "rstd", 2, [128, 1], F32)
                nmp = mkpool(ph, "nmr", 2, [128, 1], F32)
                outb = Buf(out_d, "out")
                for t in range(NTo):
                    xt = xp.next()
                    R.dma(xt.v, x_own[t * 128:(t + 1) * 128, :])
                    z = zpool.next()
                    stats = stp.next()
                    for n in range(NCH):
                        ps = psF.next()
                        for c in range(NB):
                            R.mm(ps[:, :CW], YTs[:, c, t * 128:(t + 1) * 128], Wo[:, c, n * CW:(n + 1) * CW],
                                 start=(c == 0), stop=(c == NB - 1))
                        zs = z[:, n * CW:(n + 1) * CW]
                        R.stt(zs, xt[:, n * CW:(n + 1) * CW], C.ALPHA, ps[:, :CW], ALU.mult, ALU.add)
                        so, zi = stats[:, n, :], zs
                        R.generic("vector", lambda e, a=so.ap, b=zi.ap: e.bn_stats(a, b), reads=[zs], writes=[stats.v])
                    mv, rstd, nmr = mvp.next(), rsp.next(), nmp.next()
                    R.generic("vector", lambda e, a=mv.ap, b=stats.ap: e.bn_aggr(a, b.rearrange("p n s -> p (n s)")),
                              reads=[stats.v], writes=[mv.v])
                    R.act(rstd.v, mv[:, 1:2], AF.Ln, bias=1e-5)
                    R.act(rstd.v, rstd.v, AF.Exp, scale=-0.5)
                    R.ts(nmr.v, mv[:, 0:1], rstd.v, -1.0, ALU.mult, op1=ALU.mult)
                    zn = xt
                    R.act(zn.v, z.v, AF.Identity, scale=rstd.v, bias=nmr.v)
                    R.tt(zn.v, zn.v, Grow.v, ALU.mult)
                    R.tt(zn.v, zn.v, Brow.v, ALU.add)
                    if last:
                        R.dma(outb[t * 128:(t + 1) * 128, :], zn.v)
                    else:
                        R.dma(xcur[t * 128:(t + 1) * 128, :], zn.v)
                        znb, xts = znbp.next(), xtsp.next()
                        R.copy(znb.v, zn.v, eng="gpsimd")
                        for g0_ in range(0, KC, 8):
                            ng = min(8, KC - g0_)
                            tp = trF.next()
                            for i in range(ng):
                                R.transpose(tp[:, i, :], znb[:, (g0_ + i) * 128:(g0_ + i + 1) * 128], identb.v)
                            R.copy(xts[:, g0_:g0_ + ng, :], tp[:, 0:ng, :], eng="scalar")
                        R.dma(xTo.rearrange("(kc p) s -> p kc s", p=128)[:, :, t * 128:(t + 1) * 128], xts.v)
                if last:
                    R.wait_all("sync", [outb])
            if not last:
                R.barrier()
                for xq in range(NXC):
                    ci = xTo[xq * XCH:(xq + 1) * XCH, :]
                    co = Gx[xq].rearrange("r p s -> (r p) s")
                    R.cc(lambda e, ci=ci, co=co: e.collective_compute(
                        "AllGather", ALU.bypass, replica_groups=groups, ins=[ci.opt()], outs=[co.opt()]))
                R.barrier()
        R.finalize(st)
    return nc, R


def core_layout(C, half):
    own = OWN[half]
    par = OWN[1 - half]
    blocks = own + par
    perm = np.concatenate([np.arange(b * C.QB, (b + 1) * C.QB) for b in blocks])
    return own, par, blocks, perm


def const_inputs(C, half):
    own, par, blocks, perm = core_layout(C, half)
    S, QB, TPB = C.S, C.QB, C.TPB
    pos = perm.astype(np.float32)
    d = np.arange(128)
    f128 = (10000.0 ** (-(np.arange(64, dtype=np.float32)) / np.float32(64))).astype(np.float32)
    f64 = (10000.0 ** (-(np.arange(32, dtype=np.float32)) / np.float32(32))).astype(np.float32)
    ang128 = (pos[None, :] * f128[d % 64][:, None]).astype(np.float32)
    ang64 = (pos[None, :] * f64[(d % 64) % 32][:, None]).astype(np.float32)
    rot128 = np.zeros((128, 128), np.float32)
    for dd in range(64):
        rot128[dd + 64, dd] = -1.0
        rot128[dd, dd + 64] = 1.0
    rot64 = np.zeros((128, 128), np.float32)
    for o in (0, 64):
        for dl in range(32):
            rot64[o + dl + 32, o + dl] = -1.0
            rot64[o + dl, o + dl + 32] = 1.0
    posq = perm[:C.So].reshape(C.NTo, 128).T.astype(np.float32)
    Mb = np.zeros((C.HF, 8, 8), np.float32)
    for b in range(8):
        for b2 in range(8):
            if blocks[b2] < blocks[b]:
                Mb[:, b, b2] = 1.0
    pmask = np.zeros((128, 4 * QB), np.float32)
    for j in range(4):
        if not (par[j] < own[j]):
            pmask[:, j * QB:(j + 1) * QB] = NEGB
    dmask = np.zeros((128, TPB * QB), np.float32)
    s_idx = np.arange(128)[:, None]
    t_idx = np.arange(QB)[None, :]
    for k in range(TPB):
        dmask[:, k * QB:(k + 1) * QB] = np.where(k * 128 + s_idx > t_idx, NEGB, 0.0)
    return {
        "cos128": np.cos(ang128).astype(np.float32), "sin128": np.sin(ang128).astype(np.float32),
        "cos64": np.cos(ang64).astype(np.float32), "sin64": np.sin(ang64).astype(np.float32),
        "rot128": rot128, "rot64": rot64, "ident": np.eye(128, dtype=np.float32),
        "posq": np.ascontiguousarray(posq), "posk": pos[None, :].copy(), "Mb": Mb,
        "pmask": pmask, "dmask": dmask,
    }


def fused_in_maps(C, x, mem, w_in, b_forget, w_kv, w_out, g, b, consts, ncores):
    L = w_in.shape[0]
    shared = {
        "negbf": np.ascontiguousarray(-b_forget.reshape(L, C.HF, 1)),
        "ln_g": np.ascontiguousarray(g.reshape(L, 1, -1)), "ln_b": np.ascontiguousarray(b.reshape(L, 1, -1)),
    }
    for i in range(L):
        shared[f"w_in{i}"] = np.ascontiguousarray(w_in[i])
        shared[f"w_kv{i}"] = np.ascontiguousarray(w_kv[i])
        shared[f"w_out{i}"] = np.ascontiguousarray(w_out[i])
    maps = []
    for core in range(ncores):
        bi, half = core // 2, core % 2
        own, par, blocks, perm = core_layout(C, half)
        xb = x[bi]
        sel = np.zeros((128, 2), np.float32)
        sel[:, 1 - half] = 1.0
        m = {"xT": np.ascontiguousarray(xb[perm].T), "x_own": np.ascontiguousarray(xb[perm[:C.So]]),
             "memT": np.ascontiguousarray(mem[bi].T), "sel": sel}
        m.update(shared)
        m.update(consts[half])
        maps.append(m)
    return maps


_PROG = {}


def run_fused(C, x, mem, w_in, b_forget, w_mem_kv, w_out, ln_gain, ln_bias, runner=None):
    f = lambda a: np.ascontiguousarray(np.asarray(a, dtype=np.float32))
    x, mem = f(x), f(mem)
    B = x.shape[0]
    ncores = 2 * B
    L = w_in.shape[0]
    key = (C.D, C.S, C.QB, L, ncores)
    if key not in _PROG:
        _PROG[key] = build_program(C, L=L, groups=[[2 * i, 2 * i + 1] for i in range(B)])[0]
    nc = _PROG[key]
    consts = {h: const_inputs(C, h) for h in (0, 1)}
    maps = fused_in_maps(C, x, mem, f(w_in), f(b_forget), f(w_mem_kv), f(w_out), f(ln_gain), f(ln_bias), consts, ncores)
    if runner is None:
        res = run_bass_kernel_spmd(nc, maps, core_ids=list(range(ncores))).results
    else:
        res = runner(nc, maps)
    out = np.empty_like(x)
    for core in range(ncores):
        bi, half = core // 2, core % 2
        own, par, blocks, perm = core_layout(C, half)
        out[bi][perm[:C.So]] = np.asarray(res[core]["out"], dtype=np.float32)
    return out


def kernel(x, mem, w_in, b_forget, w_mem_kv, w_out, ln_gain, ln_bias):
    return run_fused(Cfg(), x, mem, w_in, b_forget, w_mem_kv, w_out, ln_gain, ln_bias)
```
